# Optimizing a Trainium2 kernel written in Bass

```python
import math
import jax, jax.numpy as jnp
from jax import lax
import numpy as np

D_MODEL = 1024
BATCH = 4
SEQ = 4096
DEPTH = 4

GRID_W = 64
CTX_LEN = 256
CONV_WIDTH = 3
D_CONV = 512
N_DIFF_HEADS = 4
DIFF_QK_DIM = 64
DIFF_V_DIM = 128
D_ATTN = N_DIFF_HEADS * DIFF_V_DIM
N_FOURIER_GROUPS = 8
FOURIER_GROUP = D_MODEL // N_FOURIER_GROUPS
D_FF = 4 * D_MODEL
Q_BLOCK = 128
ROPE_BASE = 10000.0
LN_EPS = 1e-6
SUBLN_EPS = 1e-5
DEEPNORM_ALPHA = (2.0 * DEPTH) ** 0.25
DEEPNORM_BETA = (8.0 * DEPTH) ** -0.25
N_EVEN = (DEPTH + 1) // 2
N_ODD = DEPTH // 2
COL_CONV = 3 * D_CONV
COL_Q = N_DIFF_HEADS * 2 * DIFF_QK_DIM
COL_K = N_DIFF_HEADS * 2 * DIFF_QK_DIM
COL_V = D_ATTN
Q0 = COL_CONV
K0 = Q0 + COL_Q
V0 = K0 + COL_K
D_IN = V0 + COL_V

kernel_name = 'hybrid_conv_diffattn_fourier_dit'


def layer_norm(x):
    xf = x.astype(jnp.float32)
    mu = jnp.mean(xf, axis=-1, keepdims=True)
    var = jnp.mean(jnp.square(xf - mu), axis=-1, keepdims=True)
    return ((xf - mu) * lax.rsqrt(var + LN_EPS)).astype(x.dtype)


def layer_norm_affine(x, g, b):
    xf = x.astype(jnp.float32)
    mu = jnp.mean(xf, axis=-1, keepdims=True)
    var = jnp.mean(jnp.square(xf - mu), axis=-1, keepdims=True)
    y = (xf - mu) * lax.rsqrt(var + LN_EPS) * g.astype(jnp.float32) + b.astype(jnp.float32)
    return y.astype(x.dtype)


def modulation(cond, w, b):
    return jnp.split(jax.nn.silu(cond) @ w + b, 6, axis=-1)


def modulate(h, shift, scale):
    return layer_norm(h) * (1.0 + scale) + shift


def post_norm_residual(h, y, gate, g, b):
    return layer_norm_affine(DEEPNORM_ALPHA * h + gate * y, g, b)


def rope_1d(x, pos):
    half = x.shape[-1] // 2
    inv = ROPE_BASE ** (-jnp.arange(half, dtype=jnp.float32) / half)
    ang = pos.astype(jnp.float32)[:, None] * inv[None, :]
    cos = jnp.cos(ang)[:, None, None, :].astype(x.dtype)
    sin = jnp.sin(ang)[:, None, None, :].astype(x.dtype)
    x1, x2 = x[..., :half], x[..., half:]
    return jnp.concatenate([x1 * cos - x2 * sin, x1 * sin + x2 * cos], axis=-1)


def rope_2d(x, row, col):
    d = x.shape[-1] // 2
    return jnp.concatenate([rope_1d(x[..., :d], row), rope_1d(x[..., d:], col)], axis=-1)


def short_conv(u, w):
    n = u.shape[1]
    pad = CONV_WIDTH // 2
    up = jnp.pad(u, ((0, 0), (pad, pad), (0, 0)))
    return sum(up[:, t:t + n] * w[t] for t in range(CONV_WIDTH))


def conv_mixer(p, w):
    gb, gc, v = jnp.split(p, 3, axis=-1)
    return gb * short_conv(gc * v, w)


def heads_qk(p):
    return p.reshape(p.shape[0], p.shape[1], N_DIFF_HEADS, 2, DIFF_QK_DIM)


def heads_v(p):
    return p.reshape(p.shape[0], p.shape[1], N_DIFF_HEADS, DIFF_V_DIM)


def diff_attend(q, k, v, lam):
    s = jnp.einsum('bqhcd,bkhcd->bhcqk', q, k).astype(jnp.float32) * (DIFF_QK_DIM ** -0.5)
    p = jax.nn.softmax(s, axis=-1)
    pd = (p[:, :, 0] - lam * p[:, :, 1]).astype(v.dtype)
    return jnp.einsum('bhqk,bkhd->bqhd', pd, v)


def diff_attention_blocks(q, k, v, lam):
    b, n = q.shape[0], q.shape[1]
    nb = n // Q_BLOCK
    qb = q.reshape(b, nb, Q_BLOCK, N_DIFF_HEADS, 2, DIFF_QK_DIM).transpose(1, 0, 2, 3, 4, 5)
    out = lax.map(lambda blk: diff_attend(blk, k, v, lam), qb)
    return out.transpose(1, 0, 2, 3, 4).reshape(b, n, N_DIFF_HEADS, DIFF_V_DIM)


def sub_ln(o, g, lam_init):
    of = o.astype(jnp.float32)
    y = of * lax.rsqrt(jnp.mean(of * of, axis=-1, keepdims=True) + SUBLN_EPS)
    y = y * g.astype(jnp.float32) * (1.0 - lam_init)
    return y.reshape(o.shape[0], o.shape[1], D_ATTN).astype(o.dtype)


def fourier_mix(u):
    b, n, _ = u.shape
    ug = u.astype(jnp.float32).reshape(b, n, N_FOURIER_GROUPS, FOURIER_GROUP)
    f = jnp.fft.fft2(ug, axes=(1, 3), norm='ortho').real
    return f.reshape(b, n, D_MODEL).astype(u.dtype)


def sq_relu_mlp(u, w1, w2):
    return jnp.square(jax.nn.relu(u @ w1)) @ w2


def setup_inputs(seed: int = 0) -> dict:
    key = jax.random.key(seed)
    ks = jax.random.split(key, 16)
    f32 = jnp.float32

    def nrm(k, shape, s):
        return s * jax.random.normal(k, shape, f32)

    return {
        'x': nrm(ks[0], (BATCH, SEQ, D_MODEL), 1.0),
        'c': nrm(ks[1], (BATCH, D_MODEL), 1.0),
        'ctx': nrm(ks[2], (BATCH, CTX_LEN, D_MODEL), 1.0),
        'c_ctx': nrm(ks[3], (D_MODEL,), 1.0),
        'ada_w': nrm(ks[4], (DEPTH, D_MODEL, 6 * D_MODEL), 0.5 * D_MODEL ** -0.5),
        'ada_b': nrm(ks[5], (DEPTH, 6 * D_MODEL), 0.01),
        'ln_g': 1.0 + nrm(ks[6], (DEPTH, 2, D_MODEL), 0.05),
        'ln_b': nrm(ks[7], (DEPTH, 2, D_MODEL), 0.02),
        'mlp_w1': nrm(ks[8], (DEPTH, D_MODEL, D_FF), D_MODEL ** -0.5),
        'mlp_w2': nrm(ks[9], (DEPTH, D_FF, D_MODEL), DEEPNORM_BETA * D_FF ** -0.5),
        'w_in': nrm(ks[10], (N_EVEN, D_MODEL, D_IN), D_MODEL ** -0.5),
        'conv_w': nrm(ks[11], (N_EVEN, CONV_WIDTH, D_CONV), CONV_WIDTH ** -0.5),
        'lambda_qk': nrm(ks[12], (N_EVEN, 4, DIFF_QK_DIM), 0.1),
        'subln_g': 1.0 + nrm(ks[13], (N_EVEN, DIFF_V_DIM), 0.05),
        'w_out_mix': nrm(ks[14], (N_EVEN, D_CONV + D_ATTN, D_MODEL), DEEPNORM_BETA * (D_CONV + D_ATTN) ** -0.5),
        'w_out_fourier': nrm(ks[15], (N_ODD, D_MODEL, D_MODEL), DEEPNORM_BETA * D_MODEL ** -0.5),
    }


def reference(x, c, ctx, c_ctx, ada_w, ada_b, ln_g, ln_b, mlp_w1, mlp_w2, w_in, conv_w, lambda_qk, subln_g, w_out_mix, w_out_fourier):
    n = x.shape[1]
    rows = n // GRID_W
    row = jnp.repeat(jnp.arange(rows, dtype=jnp.int32), GRID_W)
    col = jnp.tile(jnp.arange(GRID_W, dtype=jnp.int32), rows)
    last_attn = 2 * ((DEPTH - 1) // 2)
    cond_lat = c[:, None, :]
    cond_ctx = c_ctx[None, None, :]
    h, hc = x, ctx
    for i in range(DEPTH):
        j = i // 2
        update_ctx = i < last_attn
        sh1, sc1, g1, sh2, sc2, g2 = modulation(cond_lat, ada_w[i], ada_b[i])
        u = modulate(h, sh1, sc1)
        if update_ctx or i == last_attn:
            mc = modulation(cond_ctx, ada_w[i], ada_b[i])
            uc = modulate(hc, mc[0], mc[1])
        if i % 2 == 0:
            lam_init = 0.8 - 0.6 * math.exp(-0.3 * i)
            lq = lambda_qk[j].astype(jnp.float32)
            lam = jnp.exp(jnp.sum(lq[0] * lq[1])) - jnp.exp(jnp.sum(lq[2] * lq[3])) + lam_init
            p = u @ w_in[j]
            q = rope_2d(heads_qk(p[..., Q0:K0]), row, col)
            k = rope_2d(heads_qk(p[..., K0:V0]), row, col)
            v = heads_v(p[..., V0:])
            if update_ctx:
                pc = uc @ w_in[j]
                kvc = pc[..., K0:]
            else:
                kvc = uc @ w_in[j][:, K0:]
            kc = heads_qk(kvc[..., :COL_K])
            vc = heads_v(kvc[..., COL_K:])
            k_all = jnp.concatenate([k, kc], axis=1)
            v_all = jnp.concatenate([v, vc], axis=1)
            o = diff_attention_blocks(q, k_all, v_all, lam)
            y = jnp.concatenate([conv_mixer(p[..., :COL_CONV], conv_w[j]), sub_ln(o, subln_g[j], lam_init)], axis=-1) @ w_out_mix[j]
            if update_ctx:
                oc = diff_attend(heads_qk(pc[..., Q0:K0]), kc, vc, lam)
                yc = jnp.concatenate([conv_mixer(pc[..., :COL_CONV], conv_w[j]), sub_ln(oc, subln_g[j], lam_init)], axis=-1) @ w_out_mix[j]
        else:
            y = fourier_mix(u) @ w_out_fourier[j]
            if update_ctx:
                yc = fourier_mix(uc) @ w_out_fourier[j]
        h = post_norm_residual(h, y, g1, ln_g[i, 0], ln_b[i, 0])
        h = post_norm_residual(h, sq_relu_mlp(modulate(h, sh2, sc2), mlp_w1[i], mlp_w2[i]), g2, ln_g[i, 1], ln_b[i, 1])
        if update_ctx:
            hc = post_norm_residual(hc, yc, mc[2], ln_g[i, 0], ln_b[i, 0])
            hc = post_norm_residual(hc, sq_relu_mlp(modulate(hc, mc[3], mc[4]), mlp_w1[i], mlp_w2[i]), mc[5], ln_g[i, 1], ln_b[i, 1])
    return h
```

```python
import math
import bisect
from contextlib import ExitStack
import numpy as np
import ml_dtypes
import concourse.bass as bass
import concourse.mybir as mybir
from concourse.bass_utils import run_bass_kernel_spmd

F32 = mybir.dt.float32
BF16 = mybir.dt.bfloat16
AF = mybir.ActivationFunctionType
ALU = mybir.AluOpType

D = 1024
NT = 16
NTC = 2
TOK = 2048
TOKA = 2304
DEPTH = 4
ALPHA = (2.0 * DEPTH) ** 0.25
LN_EPS = 1e-6
SUBLN_EPS = 1e-5
NKT = 34
DEBUG_ALLOC = False
SBUF_BUDGET = 192 * 1024 - 64
STOP_AFTER = None
DBG_SKIP = set()


class StopBuild(Exception):
    pass


STOPPED = [False]


def ckpt(name):
    if STOP_AFTER == name:
        STOPPED[0] = True
    return STOPPED[0]
XROWS = 1025


class Eng:
    def __init__(self, name, h, sem, is_pe=False):
        self.name, self.h, self.sem = name, h, sem
        self.insts = []
        self.marked = []
        self.markval = {}
        self.waited = {}
        self.is_pe = is_pe


class DSem:
    def __init__(self, name, ctx):
        self.name, self.ctx, self.slot = name, ctx, None

    def bind(self, kind):
        if self.slot is None:
            self.slot = self.ctx.get_slot(kind)
            self.ctx.live.append(self)

    @property
    def sem(self):
        return self.slot[0]

    @property
    def cnt(self):
        return self.slot[1]

    @cnt.setter
    def cnt(self, v):
        self.slot[1] = v


class Buf:
    __slots__ = ("w", "r", "name", "excl")

    def __init__(self, name="", excl=False):
        self.w = None
        self.r = {}
        self.name = name
        self.excl = excl


class Ctx:
    def __init__(self, nc, es):
        self.nc, self.es = nc, es
        mk = lambda n: es.enter_context(nc.semaphore(n))
        self.pe = Eng("pe", nc.tensor, mk("s_pe"), True)
        self.act = Eng("act", nc.scalar, mk("s_act"))
        self.dve = Eng("dve", nc.vector, mk("s_dve"))
        self.pool = Eng("pool", nc.gpsimd, mk("s_pool"))
        self.sync = Eng("sync", nc.sync, mk("s_sync"))
        self.engs = [self.pe, self.act, self.dve, self.pool, self.sync]
        self.dsems = []
        self.nsem = 5

    def dsem(self, name):
        return DSem(name, self)

    def get_slot(self, kind):
        if not hasattr(self, "free"):
            self.free = {"sw": [], "hw": [], "cc": []}
            self.live = []
            self.slots = []
        if self.free[kind]:
            return self.free[kind].pop()
        self.nsem += 1
        slot = [self.es.enter_context(self.nc.semaphore(f"d_{kind}{self.nsem}")), 0, kind]
        self.slots.append(slot)
        return slot

    def recycle(self):
        for d in self.live:
            if d.slot[2] != "cc":
                self.free[d.slot[2]].append(d.slot)
        self.live = [d for d in self.live if d.slot[2] == "cc"]

    def _wait_ev(self, eng, ev):
        src, idx = ev
        if isinstance(src, DSem):
            val = src.cnt
            if eng.waited.get(id(src.slot), 0) >= val:
                return
            eng.h.wait_ge(src.sem, val)
            eng.waited[id(src.slot)] = val
            return
        if src is eng and (eng.is_pe or eng is self.sync):
            return
        if eng.waited.get(id(src), -1) >= idx:
            return
        k = bisect.bisect_left(src.marked, idx)
        if k < len(src.marked):
            j = src.marked[k]
        else:
            j = len(src.insts) - 1
            self._mark(src, j)
        eng.h.wait_ge(src.sem, src.markval[j])
        eng.waited[id(src)] = j

    def _mark(self, src, j):
        assert j == len(src.insts) - 1 and j not in src.markval
        src.insts[j].then_inc(src.sem, 1)
        src.marked.append(j)
        src.markval[j] = len(src.marked)

    def _deps(self, eng, reads, writes):
        evs = []
        for b in reads:
            if b.w is not None:
                evs.append(b.w)
            if b.excl:
                for s, (src, idx) in b.r.items():
                    if src is not eng:
                        evs.append((src, idx))
        for b in writes:
            if b.w is not None:
                evs.append(b.w)
            for s, (src, idx) in b.r.items():
                evs.append((src, idx))
        for ev in evs:
            self._wait_ev(eng, ev)

    def _record(self, ev, reads, writes):
        for b in reads:
            src, idx = ev
            key = id(src.slot) if isinstance(src, DSem) else id(src)
            old = b.r.get(key)
            if old is None or old[1] < idx:
                b.r[key] = ev
        for b in writes:
            b.w = ev
            b.r = {}

    def op(self, eng, fn, reads=(), writes=(), mark=None):
        self._deps(eng, reads, writes)
        inst = fn()
        eng.insts.append(inst)
        j = len(eng.insts) - 1
        if mark is None:
            mark = not eng.is_pe
        if mark:
            self._mark(eng, j)
        self._record((eng, j), reads, writes)
        return inst

    def dma(self, q, ds, out, in_, reads=(), writes=()):
        self._deps(q, reads, writes)
        ds.bind("sw" if q is self.pool else "hw")
        inst = q.h.dma_start(out=out, in_=in_)
        ds.cnt += 16
        inst.then_inc(ds.sem, 16)
        self._record((ds, ds.cnt), reads, writes)

    def barrier(self):
        for e in self.engs:
            for s in self.engs:
                if s is e or not s.insts:
                    continue
                self._wait_ev(e, (s, len(s.insts) - 1))
            for d in getattr(self, "live", []):
                if d.cnt:
                    self._wait_ev(e, (d, d.cnt))


def _bf(a):
    return np.ascontiguousarray(a.astype(ml_dtypes.bfloat16))


def make_consts(half):
    c = {}
    c["ident_bf"] = _bf(np.eye(128, dtype=np.float32))
    c["ident_f"] = np.eye(128, dtype=np.float32)
    Rm = np.zeros((128, 128), np.float32)
    for blk in range(4):
        o = blk * 32
        for d in range(16):
            Rm[o + d, o + d + 16] = -1.0
            Rm[o + d + 16, o + d] = 1.0
    c["rotT"] = _bf(Rm.T)
    t = np.arange(TOK) + half * TOK
    row, col = t // 64, t % 64
    inv = 10000.0 ** (-np.arange(16, dtype=np.float32) / 16)
    ang = np.zeros((128, TOK), np.float32)
    for p in range(128):
        d = p % 64
        pos = row if d < 32 else col
        ang[p] = pos.astype(np.float32) * inv[d % 16]
    c["cos"] = _bf(np.cos(ang))
    c["sin"] = _bf(np.sin(ang))
    n2 = np.arange(64)[:, None].astype(np.float64)
    k1 = np.arange(64)[None, :].astype(np.float64)
    a = 2 * np.pi * n2 * k1 / 64
    c["t1"] = _bf(np.concatenate([np.cos(a), np.sin(a)], 1).astype(np.float32) / 8.0)
    n1 = np.arange(64).astype(np.float64)
    G = np.zeros((128, 64, 64), np.float32)
    for kk1 in range(64):
        k = kk1 + 64 * (np.arange(32) + 32 * half)
        a = 2 * np.pi * n1[:, None] * k[None, :] / 4096.0
        gc, gs = np.cos(a) / 8.0, np.sin(a) / 8.0
        G[0:64, kk1, 0:32] = gc
        G[64:128, kk1, 0:32] = -gs
        G[0:64, kk1, 32:64] = gs
        G[64:128, kk1, 32:64] = gc
    c["g2"] = _bf(G)
    ch = np.arange(128).astype(np.float64)
    a = 2 * np.pi * ch[:, None] * ch[None, :] / 128.0
    cs = np.stack([np.cos(a), -np.sin(a)], 1) / math.sqrt(128.0)
    c["ccs"] = _bf(cs.astype(np.float32))
    n = np.arange(256).astype(np.float64)
    a = 2 * np.pi * n[:, None] * n[None, :] / 256.0
    m = np.stack([np.cos(a), np.sin(a)], 1) / 16.0
    c["c256"] = _bf(m.reshape(2, 128, 2, 256).transpose(1, 0, 2, 3).astype(np.float32))
    c["halo_mask"] = np.tile(np.array([[float(half), float(1 - half)]], np.float32), (128, 1))
    c["ones_f"] = np.ones((64, 128), np.float32)
    return c


CONST_SPECS = {
    "ident_bf": ([128, 128], BF16), "ident_f": ([128, 128], F32), "rotT": ([128, 128], BF16),
    "cos": ([128, TOK], BF16), "sin": ([128, TOK], BF16), "t1": ([64, 128], BF16),
    "g2": ([128, 64, 64], BF16), "ccs": ([128, 2, 128], BF16), "c256": ([128, 2, 2, 256], BF16),
    "halo_mask": ([128, 2], F32), "ones_f": ([64, 128], F32),
}

W_SPECS = {
    "ada_w": [DEPTH, D, 6 * D], "ada_b": [DEPTH, 6 * D], "ln_g": [DEPTH, 2, D], "ln_b": [DEPTH, 2, D],
    "mlp_w1": [DEPTH, D, 4 * D], "mlp_w2": [DEPTH, 4 * D, D], "w_in": [2, D, 3072], "conv_w": [2, 3, 512],
    "lambda_qk": [2, 4, 64], "subln_g": [2, 128], "w_out_mix": [2, D, D], "w_out_fourier": [2, D, D],
}


class Prog:
    def __init__(self, layers):
        self.layers = layers
        nc = self.nc = bass.Bass("TRN2", target_bir_lowering=False)
        self.es = ExitStack()
        self.K = Ctx(nc, self.es)
        K = self.K
        dt = lambda n, s, d, k: nc.dram_tensor(n, s, d, kind=k)
        self.x_in = dt("x_in", [TOK, D], F32, "ExternalInput")
        self.hc_in = dt("hc_in", [256, D], F32, "ExternalInput")
        self.cvec = dt("cvec", [2, D], F32, "ExternalInput")
        self.h_out = dt("h_out", [TOK, D], F32, "ExternalOutput")
        self.hc_out = dt("hc_out", [256, D], F32, "ExternalOutput")
        self.W = {}
        for n, s in W_SPECS.items():
            per = 2 if n in ("w_in", "conv_w", "lambda_qk", "subln_g", "w_out_mix", "w_out_fourier") else 1
            for i in self.layers:
                idx = i // per
                if n == "w_out_fourier" and i % 2 == 0:
                    continue
                if n in ("w_in", "conv_w", "lambda_qk", "subln_g", "w_out_mix") and i % 2 == 1:
                    continue
                self.W[(n, i)] = dt(f"{n}_{idx}", s[1:], F32, "ExternalInput")
        self.Cd = {n: dt(n, s, d, "ExternalInput") for n, (s, d) in CONST_SPECS.items()}
        self.xk_in = dt("xk_in", [512, TOK], BF16, "Internal")
        self.xk_out = dt("xk_out", [1024, TOK], BF16, "Internal")
        self.xv_in = dt("xv_in", [512, TOK], BF16, "Internal")
        self.xv_out = dt("xv_out", [1024, TOK], BF16, "Internal")
        self.xh_in = dt("xh_in", [1, TOK], BF16, "Internal")
        self.xh_out = dt("xh_out", [2, TOK], BF16, "Internal")
        self.fin = [dt(f"fin{k}", [1024, D], BF16, "Internal") for k in range(2)]
        self.fout = [dt(f"fout{k}", [2048, D], BF16, "Internal") for k in range(2)]
        self.yd = dt("yd", [128, 64 * D], BF16, "Internal")
        self.cmd = dt("cmd", [128, 4 * TOKA], BF16, "Internal")
        self.b_cmd = Buf("cmd")
        self.b_xin, self.b_xout, self.b_fin, self.b_fout, self.b_yd = (Buf(n) for n in ("xin", "xout", "fin", "fout", "yd"))
        self.csem = K.dsem("cc")
        self.uid = 0
        with nc.allow_non_contiguous_dma(reason="small strided parameter loads"):
            self.build()
        self.es.close()

    def sb(self, st, shape, dtype, name=None):
        self.uid += 1
        t = st.enter_context(self.nc.sbuf_tensor(f"{name or 't'}_{self.uid}", shape, dtype))
        nb = int(np.prod(shape[1:])) * (4 if dtype == F32 else 2)
        nb = (nb + 31) // 32 * 32
        self.cur = getattr(self, "cur", 0) + nb
        self.peak = max(getattr(self, "peak", 0), self.cur)
        assert self.cur <= SBUF_BUDGET, f"SBUF budget exceeded at {name}: {self.cur}"

        def _dec(nb=nb):
            self.cur -= nb
        st.callback(_dec)
        return t

    def scr(self, st, name, shape, dtype):
        if not hasattr(st, "_scr"):
            st._scr = {}
        if name not in st._scr:
            st._scr[name] = (self.sb(st, shape, dtype, name), Buf(name))
        return st._scr[name]

    def collective(self, inp, out, b_in, b_out):
        K = self.K
        q = K.pool
        K._deps(q, [b_in], [b_out])
        self.csem.bind("cc")
        inst = self.nc.gpsimd.collective_compute(
            "AllGather", ALU.bypass, replica_groups=[[0, 1], [2, 3], [4, 5], [6, 7]],
            ins=[inp], outs=[out])
        self.csem.cnt += 1
        inst.then_inc(self.csem.sem, 1)
        K._record((self.csem, self.csem.cnt), [b_in], [b_out])

    def build(self):
        nc, K, es = self.nc, self.K, self.es
        st = es
        self.pb = [es.enter_context(nc.psum_tensor(f"pb{i}", [128, 512], F32)) for i in range(6)]
        self.pbb = [es.enter_context(nc.psum_tensor(f"pbb{i}", [128, 1024], BF16)) for i in range(2)]
        self.b_pb = [Buf(f"pb{i}", excl=True) for i in range(6)]
        self.b_pbb = [Buf(f"pbb{i}", excl=True) for i in range(2)]
        self.h = self.sb(st, [128, NT + NTC, D], F32, "h")
        self.b_h = [Buf(f"h{i}") for i in range(NT + NTC)]
        self.cn = {}
        self.b_cn = {}
        dsc = K.dsem("consts")
        for n in ("ident_bf", "ident_f", "rotT", "t1", "ccs", "halo_mask", "ones_f"):
            s, d = CONST_SPECS[n]
            self.cn[n] = self.sb(st, s, d, n)
            self.b_cn[n] = Buf(n)
            K.dma(K.sync, dsc, self.cn[n][:], self.Cd[n].ap(), [], [self.b_cn[n]])
        self.scl = self.sb(st, [128, 8, 64], F32, "scl")
        self.b_scl = Buf("scl")
        ctmp = self.sb(st, [128, 2, 8], F32, "ctmp")
        b_ctmp = Buf()
        K.dma(K.sync, dsc, ctmp[:], self.cvec.ap().rearrange("r (k p) -> p r k", p=128), [], [b_ctmp])
        K.op(K.pool, lambda: nc.gpsimd.memset(self.scl[:], 0.0), [], [self.b_scl])
        for r in range(2):
            K.op(K.act, lambda r=r: nc.scalar.activation(out=self.scl[:, :, 32 * r], in_=ctmp[:, r, :], func=AF.Silu),
                 [b_ctmp], [self.b_scl])
        dsx = K.dsem("xload")
        first = self.layers[0]
        for t in range(NT):
            K.dma(K.sync, dsx, self.h[:, t, :], self.x_in.ap()[t * 128:(t + 1) * 128, :], [], [self.b_h[t]])
        for t in range(NTC):
            K.dma(K.sync, dsx, self.h[:, NT + t, :], self.hc_in.ap()[t * 128:(t + 1) * 128, :], [], [self.b_h[NT + t]])
        STOPPED[0] = False
        if not ckpt("load"):
            for i in self.layers:
                if not STOPPED[0]:
                    self.layer(i)
                    K.barrier()
                    K.recycle()
        K.barrier()
        dso = K.dsem("store")
        for t in range(NT):
            K.dma(K.sync, dso, self.h_out.ap()[t * 128:(t + 1) * 128, :], self.h[:, t, :], [self.b_h[t]], [])
        for t in range(NTC):
            K.dma(K.sync, dso, self.hc_out.ap()[t * 128:(t + 1) * 128, :], self.h[:, NT + t, :], [self.b_h[NT + t]], [])
        K.sync.h.wait_ge(dso.sem, dso.cnt)
        K.barrier()

    def alloc_mods(self, st, st_m, use_ctx, bc_all):
        out = {}
        sufs = ["", "c"] if use_ctx else [""]
        for suf in sufs:
            out["g" + suf] = (self.sb(st, [128, D], F32, "bc_g" + suf), Buf())
        for suf in sufs:
            for nm in ("sh", "scp"):
                if bc_all:
                    out[nm + suf] = (self.sb(st_m, [128, D], F32, "bc_" + nm + suf), Buf())
                else:
                    out[nm + suf] = (self.sb(st_m, [128, 8], F32, "col_" + nm + suf), Buf())
        out["_bc_all"] = bc_all
        out["_sufs"] = sufs
        return out

    def mods(self, i, sub, out):
        nc, K = self.nc, self.K
        bc_all = out["_bc_all"]
        with ExitStack() as ls:
            modrow = self.sb(ls, [64, 3072], F32, "modrow")
            b_modrow = Buf()
            brow = self.sb(ls, [64, 3072], F32, "brow")
            b_brow = Buf()
            slots = [self.sb(ls, [128, 8, 256], F32, "adas") for _ in range(2)]
            b_slots = [Buf(), Buf()]
            ds = [K.dsem(f"ada{i}{sub}a"), K.dsem(f"ada{i}{sub}b")]
            dsb = K.dsem(f"adab{i}{sub}")
            c0 = (sub - 1) * 3072
            for r in (0, 32):
                K.dma(K.sync, dsb, brow[r:r + 1, :], self.W[("ada_b", i)].ap().rearrange("(o c) -> o c", o=1)[0:1, c0:c0 + 3072], [], [b_brow])

            def load_chunk(n):
                s = n % 2
                src = self.W[("ada_w", i)].ap()[:, c0 + n * 256:c0 + (n + 1) * 256].rearrange("(k p) c -> p k c", p=128)
                K.dma(K.sync, ds[s], slots[s][:], src, [], [b_slots[s]])

            load_chunk(0)
            for n in range(12):
                s = n % 2
                if n + 1 < 12:
                    load_chunk(n + 1)
                pbk = n % 2
                for k in range(8):
                    K.op(K.pe, lambda k=k, pbk=pbk, s=s: nc.tensor.matmul(
                        self.pb[pbk][0:33, 0:256], lhsT=self.scl[:, k, 0:33], rhs=slots[s][:, k, :],
                        start=(k == 0), stop=(k == 7)),
                        [self.b_scl, b_slots[s]], [self.b_pb[pbk]], mark=(k == 7))
                for r in (0, 32):
                    K.op(K.dve, lambda n=n, pbk=pbk, r=r: nc.vector.tensor_tensor(
                        out=modrow[r:r + 1, n * 256:(n + 1) * 256], in0=self.pb[pbk][r:r + 1, 0:256], in1=brow[r:r + 1, n * 256:(n + 1) * 256],
                        op=ALU.add), [self.b_pb[pbk], b_brow], [b_modrow])
            for r in (0, 32):
                K.op(K.dve, lambda r=r: nc.vector.tensor_scalar(out=modrow[r:r + 1, 1024:2048], in0=modrow[r:r + 1, 1024:2048],
                                                                 scalar1=1.0, scalar2=None, op0=ALU.add), [b_modrow], [b_modrow])
            names = ["sh", "scp", "g"]
            for suf in out["_sufs"]:
                r = 32 if suf == "c" else 0
                for v in range(3):
                    t, b = out[names[v] + suf]
                    if v == 2 or bc_all:
                        for hf in range(2):
                            pbk = 2 + hf
                            K.op(K.pe, lambda r=r, v=v, hf=hf, pbk=pbk: nc.tensor.matmul(
                                self.pb[pbk][:, :], lhsT=self.cn["ones_f"][r:r + 1, :],
                                rhs=modrow[r:r + 1, v * 1024 + hf * 512:v * 1024 + (hf + 1) * 512], start=True, stop=True),
                                [self.b_cn["ones_f"], b_modrow], [self.b_pb[pbk]], mark=True)
                            K.op(K.act, lambda t=t, hf=hf, pbk=pbk: nc.scalar.copy(out=t[:, hf * 512:(hf + 1) * 512], in_=self.pb[pbk][:, :]),
                                 [self.b_pb[pbk]], [b])
                    else:
                        pbk = 4 + (v % 2)
                        for k in range(8):
                            K.op(K.pe, lambda r=r, v=v, k=k, pbk=pbk: nc.tensor.matmul(
                                self.pb[pbk][:, k:k + 1], lhsT=modrow[r:r + 1, v * 1024 + k * 128:v * 1024 + (k + 1) * 128],
                                rhs=self.cn["ones_f"][r:r + 1, 0:1], start=True, stop=True, skip_group_check=True),
                                [self.b_cn["ones_f"], b_modrow], [self.b_pb[pbk]], mark=(k == 7))
                        K.op(K.act, lambda t=t, pbk=pbk: nc.scalar.copy(out=t[:, :], in_=self.pb[pbk][:, 0:8]), [self.b_pb[pbk]], [b])
            K.barrier()

    def ln_bc(self, st, i, sub):
        nc, K = self.nc, self.K
        ds = K.dsem(f"lnbc{i}{sub}")
        out = {}
        for nm in ("ln_g", "ln_b"):
            t = self.sb(st, [128, D], F32, nm)
            b = Buf()
            row = self.W[(nm, i)].ap()[sub - 1:sub, :]
            src = bass.AP(row.tensor, row.offset, [[0, 128], [1, D]])
            K.dma(K.sync, ds, t[:], src, [], [b])
            out[nm] = (t, b)
        return out

    def ln_stats(self, st, src, b_src, slot=0):
        nc, K = self.nc, self.K
        stats, b = self.scr(st, f"stats{slot}", [128, 2, 6], F32)
        mv, _ = self.scr(st, f"mv{slot}", [128, 2], F32)
        rstd, _ = self.scr(st, f"rstd{slot}", [128, 2], F32)
        for hf in range(2):
            K.op(K.dve, lambda hf=hf: nc.vector.bn_stats(out=stats[:, hf, :], in_=src[:, hf * 512:(hf + 1) * 512]), [b_src], [b])
        K.op(K.dve, lambda: nc.vector.bn_aggr(out=mv[:], in_=stats[:].rearrange("p a b -> p (a b)")), [b], [b])
        K.op(K.pool, lambda: nc.gpsimd.tensor_scalar(out=rstd[:, 0:1], in0=mv[:, 1:2], scalar1=1.0, scalar2=LN_EPS, op0=ALU.mult, op1=ALU.add), [b], [b])
        K.op(K.pool, lambda: nc.gpsimd.tensor_tensor(out=rstd[:, 0:1], in0=rstd[:, 0:1], in1=self.eps_t[:, 2:3], op=ALU.pow), [b, self.b_eps], [b])
        K.op(K.dve, lambda: nc.vector.tensor_scalar(out=rstd[:, 1:2], in0=mv[:, 0:1], scalar1=rstd[:, 0:1], scalar2=-1.0,
                                                     op0=ALU.mult, op1=ALU.mult), [b], [b])
        return mv, rstd, b

    def modulate_tok(self, ls, t, bc, suf, out_ap, b_out):
        nc, K = self.nc, self.K
        src = self.h[:, t, :]
        mv, rstd, b = self.ln_stats(ls, src, self.b_h[t])
        tmp, bt = self.scr(ls, "modtmp", [128, D], F32)
        scp, b_scp = bc["scp" + suf]
        sh, b_sh = bc["sh" + suf]
        K.op(K.dve, lambda: nc.vector.scalar_tensor_tensor(out=tmp[:], in0=src, scalar=mv[:, 0:1], in1=scp[:],
                                                            op0=ALU.subtract, op1=ALU.mult), [self.b_h[t], b, b_scp], [bt])
        K.op(K.dve, lambda: nc.vector.scalar_tensor_tensor(out=out_ap, in0=tmp[:], scalar=rstd[:, 0:1], in1=sh[:],
                                                            op0=ALU.mult, op1=ALU.add), [bt, b, b_sh], [b_out])

    def modulate_T(self, ls, t, bc, suf, xb, b_xb, dst, b_dst, col0, pbk):
        nc, K = self.nc, self.K
        src = self.h[:, t, :]
        mv, rstd, b = self.ln_stats(ls, src, self.b_h[t], slot=pbk)
        K.op(K.act, lambda: nc.scalar.activation(out=xb[:], in_=src, func=AF.Identity, bias=rstd[:, 1:2], scale=rstd[:, 0:1]),
             [self.b_h[t], b], [b_xb])
        scp, b_scp = bc["scp" + suf]
        sh, b_sh = bc["sh" + suf]
        for k in range(8):
            K.op(K.pe, lambda k=k: nc.tensor.transpose(out=self.pbb[pbk][:, k * 128:(k + 1) * 128], in_=xb[:, k * 128:(k + 1) * 128],
                                                        identity=self.cn["ident_bf"][:]),
                 [b_xb, self.b_cn["ident_bf"]], [self.b_pbb[pbk]], mark=(k == 7))
        for k in range(8):
            K.op(K.act, lambda k=k: nc.scalar.activation(out=dst[:, k, col0:col0 + 128], in_=self.pbb[pbk][:, k * 128:(k + 1) * 128],
                                                          func=AF.Identity, bias=sh[:, k:k + 1], scale=scp[:, k:k + 1]),
                 [self.b_pbb[pbk], b_scp, b_sh], [b_dst])

    def residual_ln(self, ls, t, y_aps, b_ys, bc, suf, lnb, first=True, last=True):
        nc, K = self.nc, self.K
        g, b_g = bc["g" + suf]
        hh = self.h[:, t, :]
        tmp, bt = self.scr(ls, "restmp", [128, D], F32)
        for hf in range(2):
            K.op(K.dve, lambda hf=hf: nc.vector.tensor_tensor(out=tmp[:, hf * 512:(hf + 1) * 512], in0=y_aps[hf],
                                                               in1=g[:, hf * 512:(hf + 1) * 512], op=ALU.mult),
                 [b_ys[hf], b_g], [bt])
        K.op(K.dve, lambda: nc.vector.scalar_tensor_tensor(out=hh, in0=hh, scalar=(ALPHA if first else 1.0), in1=tmp[:],
                                                            op0=ALU.mult, op1=ALU.add), [bt, self.b_h[t]], [self.b_h[t]])
        if not last:
            return
        mv, rstd, b = self.ln_stats(ls, hh, self.b_h[t])
        lg, b_lg = lnb["ln_g"]
        lb, b_lb = lnb["ln_b"]
        K.op(K.dve, lambda: nc.vector.scalar_tensor_tensor(out=tmp[:], in0=hh, scalar=mv[:, 0:1], in1=lg[:],
                                                            op0=ALU.subtract, op1=ALU.mult), [self.b_h[t], b, b_lg], [bt])
        K.op(K.dve, lambda: nc.vector.scalar_tensor_tensor(out=hh, in0=tmp[:], scalar=rstd[:, 0:1], in1=lb[:],
                                                            op0=ALU.mult, op1=ALU.add), [bt, b, b_lb], [self.b_h[t]])

    def layer(self, i):
        nc, K = self.nc, self.K
        j = i // 2
        ctx_upd = i < 2
        ctx_in = i <= 2
        ntl = NT + (NTC if ctx_in else 0)
        ntu = NT + (NTC if ctx_upd else 0)
        if not hasattr(self, "eps_t"):
            self.eps_t = self.sb(self.es, [128, 4], F32, "eps")
            self.b_eps = Buf()
            K.op(K.pool, lambda: nc.gpsimd.memset(self.eps_t[:, 0:1], LN_EPS), [], [self.b_eps])
            K.op(K.pool, lambda: nc.gpsimd.memset(self.eps_t[:, 1:2], SUBLN_EPS), [], [self.b_eps])
            K.op(K.pool, lambda: nc.gpsimd.memset(self.eps_t[:, 2:3], -0.5), [], [self.b_eps])
            K.barrier()
        with ExitStack() as lst:
            if i % 2 == 0:
                bc = self.alloc_mods(lst, lst, ctx_in, bc_all=False)
                self.mods(i, 1, bc)
                if ckpt("mods"):
                    return
                self.even_mixer(lst, i, j, bc, ctx_upd, ntl, ntu)
            else:
                self.odd_mixer(lst, i, j, ctx_upd, ntl, ntu)
            K.barrier()
        if STOPPED[0] or ckpt("mixer"):
            return
        with ExitStack() as lst:
            bc = self.alloc_mods(lst, lst, ctx_upd, bc_all=False)
            self.mods(i, 2, bc)
            lnb = self.ln_bc(lst, i, 2)
            self.mlp(lst, i, bc, lnb, ntu)
            K.barrier()

    def build_uT(self, ntl, bc, uT, b_uT):
        K = self.K
        with ExitStack() as ls:
            xb = [self.sb(ls, [128, D], BF16, "xb") for _ in range(2)]
            b_xb = [Buf(), Buf()]
            for t in range(ntl):
                if True:
                    l2 = ls
                    s = t % 2
                    self.modulate_T(l2, t, bc, "c" if t >= NT else "", xb[s], b_xb[s], uT, b_uT[t], t * 128, s)
            K.barrier()

    def mlp(self, st, i, bc, lnb, ntu):
        nc, K = self.nc, self.K
        ncols = ntu * 128
        uT = self.sb(st, [128, 8, ncols], BF16, "uT")
        b_uT = [Buf() for _ in range(ntu)]
        self.build_uT(ntu, bc, uT, b_uT)
        ring = [self.sb(st, [128, 8, 1024], BF16, "wring") for _ in range(3)]
        b_ring = [Buf() for _ in range(3)]
        dsw = [K.dsem(f"mlpw{i}{k}") for k in range(3)]
        h1 = self.sb(st, [128, 8, 512], BF16, "h1")
        b_h1 = Buf()
        groups = []
        t0 = 0
        while t0 < ntu:
            n = min(4, ntu - t0)
            groups.append((t0, n))
            t0 += n

        def load(k):
            q, which = k // 2, k % 2
            if q >= 4:
                return
            s = k % 3
            if which == 0:
                src_ = self.W[("mlp_w1", i)].ap()[:, q * 1024:(q + 1) * 1024].rearrange("(k p) c -> p k c", p=128)
            else:
                src_ = self.W[("mlp_w2", i)].ap()[q * 1024:(q + 1) * 1024, :].rearrange("(k p) c -> p k c", p=128)
            K.dma(K.pool, dsw[s], ring[s][:], src_, [], [b_ring[s]])

        load(0)
        load(1)
        load(2)
        for q in range(4):
            w1, bw1 = ring[(2 * q) % 3], b_ring[(2 * q) % 3]
            w2, bw2 = ring[(2 * q + 1) % 3], b_ring[(2 * q + 1) % 3]
            for gidx, (t0, n) in enumerate(groups):
                w = n * 128
                for f in range(8):
                    pbk = f % 2
                    for k in range(8):
                        K.op(K.pe, lambda f=f, k=k, pbk=pbk: nc.tensor.matmul(
                            self.pb[pbk][:, 0:w], lhsT=w1[:, k, f * 128:(f + 1) * 128], rhs=uT[:, k, t0 * 128:t0 * 128 + w],
                            start=(k == 0), stop=(k == 7)),
                            [bw1] + b_uT[t0:t0 + n], [self.b_pb[pbk]], mark=(k == 7))
                    K.op(K.act, lambda f=f, pbk=pbk: nc.scalar.activation(out=h1[:, f, 0:w], in_=self.pb[pbk][:, 0:w], func=AF.Relu),
                         [self.b_pb[pbk]], [b_h1])
                    K.op(K.pool, lambda f=f: nc.gpsimd.tensor_tensor(out=h1[:, f, 0:w], in0=h1[:, f, 0:w], in1=h1[:, f, 0:w], op=ALU.mult), [b_h1], [b_h1])
                if gidx == len(groups) - 1:
                    load(2 * q + 3)
                for tt in range(n):
                    t = t0 + tt
                    pk = [2 + 2 * (t % 2), 3 + 2 * (t % 2)]
                    for hf in range(2):
                        for f in range(8):
                            K.op(K.pe, lambda f=f, hf=hf, tt=tt: nc.tensor.matmul(
                                self.pb[pk[hf]][:, :], lhsT=h1[:, f, tt * 128:(tt + 1) * 128], rhs=w2[:, f, hf * 512:(hf + 1) * 512],
                                start=(f == 0), stop=(f == 7)),
                                [b_h1, bw2], [self.b_pb[pk[hf]]], mark=(f == 7))
                    self.residual_ln(st, t, [self.pb[pk[0]][:, :], self.pb[pk[1]][:, :]], [self.b_pb[pk[0]], self.b_pb[pk[1]]],
                                     bc, "c" if t >= NT else "", lnb, first=(q == 0), last=(q == 3))
            load(2 * q + 4)

    def even_mixer(self, st, i, j, bc, ctx_upd, ntl, ntu):
        nc, K = self.nc, self.K
        ncols = ntl * 128
        lam_init = 0.8 - 0.6 * math.exp(-0.3 * i)
        QT = self.sb(st, [128, 4, TOKA], BF16, "QT")
        b_QT = Buf()
        slT = self.sb(st, [128, 4, TOKA], BF16, "slT")
        b_sl = Buf()
        kc = self.sb(st, [128, 4, 256], BF16, "kc")
        vc = self.sb(st, [128, 2, 512], BF16, "vc")
        b_kc, b_vc = Buf(), Buf()
        small = self.sb(st, [128, 64], F32, "small")
        b_small = Buf()
        cw = self.sb(st, [128, 4, 3], F32, "cw")
        b_cw = Buf()
        gbh = self.sb(st, [128, 4, 2], F32, "gbh")
        b_gbh = Buf()
        dsm = K.dsem(f"evsm{i}")
        for tap in range(3):
            K.dma(K.sync, dsm, cw[:, :, tap], self.W[("conv_w", i)].ap()[tap, :].rearrange("(c p) -> p c", p=128), [], [b_cw])
        lq = self.sb(st, [128, 4, 64], F32, "lq")
        b_lq = Buf()
        lsrc = self.W[("lambda_qk", i)].ap()
        K.dma(K.sync, dsm, lq[:], bass.AP(lsrc.tensor, lsrc.offset, [[0, 128], [64, 4], [1, 64]]), [], [b_lq])
        sg = self.sb(st, [128, 128], F32, "sg")
        b_sg = Buf()
        ssrc = self.W[("subln_g", i)].ap()
        K.dma(K.sync, dsm, sg[:], bass.AP(ssrc.tensor, ssrc.offset, [[0, 128], [1, 128]]), [], [b_sg])
        ltmp = self.sb(st, [128, 2, 64], F32, "ltmp")
        b_lt = Buf()
        for a in range(2):
            K.op(K.dve, lambda a=a: nc.vector.tensor_tensor(out=ltmp[:, a, :], in0=lq[:, 2 * a, :], in1=lq[:, 2 * a + 1, :], op=ALU.mult),
                 [b_lq], [b_lt])
            K.op(K.dve, lambda a=a: nc.vector.tensor_reduce(out=small[:, a:a + 1], in_=ltmp[:, a, :], op=ALU.add,
                                                             axis=mybir.AxisListType.X), [b_lt], [b_small])
        K.op(K.act, lambda: nc.scalar.activation(out=small[:, 2:4], in_=small[:, 0:2], func=AF.Exp), [b_small], [b_small])
        K.op(K.dve, lambda: nc.vector.scalar_tensor_tensor(out=small[:, 4:5], in0=small[:, 3:4], scalar=-lam_init, in1=small[:, 2:3],
                                                            op0=ALU.add, op1=ALU.subtract), [b_small], [b_small])
        K.op(K.dve, lambda: nc.vector.tensor_scalar(out=sg[:], in0=sg[:], scalar1=(1.0 - lam_init), scalar2=None, op0=ALU.mult),
             [b_sg], [b_sg])

        groups = [(g * 4, 4) for g in range(4)] + ([(16, 2)] if ntl > NT else [])
        with ExitStack() as sa:
            uT = self.sb(sa, [128, 8, ncols], BF16, "uT")
            b_uT = [Buf() for _ in range(ntl)]
            self.build_uT(ntl, bc, uT, b_uT)
            if ckpt("uT"):
                return
            with ExitStack() as ls:
                gv = self.sb(ls, [128, 4, TOKA + 4], BF16, "gv")
                b_gv = Buf()
                K.op(K.pool, lambda: nc.gpsimd.memset(gv[:], 0.0), [], [b_gv])
                cnt = 0
                for hh in range(2):
                    with ExitStack() as l3:
                        wc = self.sb(l3, [128, 8, 2, 256], BF16, "wconv")
                        b_wc = Buf()
                        dsw = K.dsem(f"wconv{i}{hh}")
                        for blk in range(2):
                            c_lo = 512 + blk * 512 + hh * 256
                            K.dma(K.pool, dsw, wc[:, :, blk, :], self.W[("w_in", i)].ap()[:, c_lo:c_lo + 256].rearrange("(k p) c -> p k c", p=128),
                                  [], [b_wc])
                        gcT = [self.sb(l3, [128, 512], BF16, "gcT") for _ in range(2)]
                        b_gc = [Buf(), Buf()]
                        for (t0, n) in groups:
                            w = n * 128
                            c0 = t0 * 128 + 1 + (2 if t0 >= NT else 0)
                            for c in (2 * hh, 2 * hh + 1):
                                s = cnt % 2
                                cnt += 1
                                for blk, pbk in ((0, 0), (1, 1)):
                                    for k in range(8):
                                        K.op(K.pe, lambda k=k, blk=blk, pbk=pbk, c=c: nc.tensor.matmul(
                                            self.pb[pbk][:, 0:w], lhsT=wc[:, k, blk, (c % 2) * 128:(c % 2 + 1) * 128],
                                            rhs=uT[:, k, t0 * 128:t0 * 128 + w], start=(k == 0), stop=(k == 7)),
                                            [b_wc] + b_uT[t0:t0 + n], [self.b_pb[pbk]], mark=(k == 7))
                                K.op(K.act, lambda s=s: nc.scalar.copy(out=gcT[s][:, 0:w], in_=self.pb[0][:, 0:w]), [self.b_pb[0]], [b_gc[s]])
                                K.op(K.dve, lambda s=s, c=c, c0=c0: nc.vector.tensor_tensor(out=gv[:, c, c0:c0 + w], in0=self.pb[1][:, 0:w], in1=gcT[s][:, 0:w],
                                                                                             op=ALU.mult), [self.b_pb[1], b_gc[s]], [b_gv])
                        K.barrier()
                dsh = K.dsem(f"halo{i}")
                hrow = self.xh_in.ap()[0:1, 0:1024]
                for wi, colv in ((0, 1), (1, TOK), (2, 1), (3, TOK)):
                    dst = bass.AP(hrow.tensor, hrow.offset + wi * 512, [[1, 128], [128, 4], [1, 1]])
                    K.dma(K.sync, dsh, dst, gv[:, :, colv:colv + 1], [b_gv], [self.b_xin])
                with ExitStack() as l3:
                    wc = self.sb(l3, [128, 8, 512], BF16, "wgb")
                    b_wc = Buf()
                    dsw = K.dsem(f"wgb{i}")
                    K.dma(K.pool, dsw, wc[:], self.W[("w_in", i)].ap()[:, 0:512].rearrange("(k p) c -> p k c", p=128), [], [b_wc])
                    ctmp = [self.sb(l3, [128, 512], F32, "ctmp")] * 2
                    b_ct = [Buf()] * 2
                    cms = [self.sb(l3, [128, 512], BF16, "cms")] * 2
                    b_cms = [Buf()] * 2
                    dscm = K.dsem(f"cmd{i}")
                    for (t0, n) in groups:
                        w = n * 128
                        c0 = t0 * 128 + 1 + (2 if t0 >= NT else 0)
                        for c in range(4):
                            s = cnt % 2
                            cnt += 1
                            pbk = 2 + s
                            for k in range(8):
                                K.op(K.pe, lambda k=k, pbk=pbk, c=c: nc.tensor.matmul(
                                    self.pb[pbk][:, 0:w], lhsT=wc[:, k, c * 128:(c + 1) * 128],
                                    rhs=uT[:, k, t0 * 128:t0 * 128 + w], start=(k == 0), stop=(k == 7)),
                                    [b_wc] + b_uT[t0:t0 + n], [self.b_pb[pbk]], mark=(k == 7))
                            K.op(K.dve, lambda s=s, c=c, c0=c0: nc.vector.tensor_scalar(out=ctmp[s][:, 0:w], in0=gv[:, c, c0:c0 + w], scalar1=cw[:, c, 1:2],
                                                                                         scalar2=None, op0=ALU.mult), [b_gv, b_cw], [b_ct[s]])
                            K.op(K.dve, lambda s=s, c=c, c0=c0: nc.vector.scalar_tensor_tensor(out=ctmp[s][:, 0:w], in0=gv[:, c, c0 - 1:c0 - 1 + w], scalar=cw[:, c, 0:1],
                                                                                                in1=ctmp[s][:, 0:w], op0=ALU.mult, op1=ALU.add), [b_gv, b_cw, b_ct[s]], [b_ct[s]])
                            K.op(K.dve, lambda s=s, c=c, c0=c0: nc.vector.scalar_tensor_tensor(out=ctmp[s][:, 0:w], in0=gv[:, c, c0 + 1:c0 + 1 + w], scalar=cw[:, c, 2:3],
                                                                                                in1=ctmp[s][:, 0:w], op0=ALU.mult, op1=ALU.add), [b_gv, b_cw, b_ct[s]], [b_ct[s]])
                            K.op(K.dve, lambda s=s, c=c, pbk=pbk: nc.vector.tensor_tensor(out=cms[s][:, 0:w], in0=self.pb[pbk][:, 0:w], in1=ctmp[s][:, 0:w],
                                                                                           op=ALU.mult), [self.b_pb[pbk], b_ct[s]], [b_cms[s]])
                            K.dma(K.sync, dscm, self.cmd.ap()[:, c * TOKA + t0 * 128:c * TOKA + t0 * 128 + w], cms[s][:, 0:w], [b_cms[s]], [self.b_cmd])
                            if t0 == 0:
                                K.op(K.act, lambda c=c, pbk=pbk: nc.scalar.copy(out=gbh[:, c, 0:1], in_=self.pb[pbk][:, 0:1]), [self.b_pb[pbk]], [b_gbh])
                            if t0 == 12:
                                K.op(K.act, lambda c=c, pbk=pbk: nc.scalar.copy(out=gbh[:, c, 1:2], in_=self.pb[pbk][:, 511:512]), [self.b_pb[pbk]], [b_gbh])
                    K.barrier()
            if ckpt("A2"):
                return
            dsx = K.dsem(f"xst{i}")
            with ExitStack() as ls:
                cs = [self.sb(ls, [128, TOK], BF16, n) for n in ("cos", "sin")]
                b_cs = Buf()
                dscs = K.dsem(f"cs{i}")
                for a, n in enumerate(("cos", "sin")):
                    K.dma(K.sync, dscs, cs[a][:], self.Cd[n].ap(), [], [b_cs])
                stg = self.sb(ls, [128, 4, 512], BF16, "stg")
                b_stg = Buf()
                qb = [self.sb(ls, [128, 512], BF16, "qb") for _ in range(2)]
                b_qb = [Buf(), Buf()]
                r1 = self.sb(ls, [128, 512], F32, "r1")
                b_r = Buf()
                cnt = 0
                for blk in range(2):
                    with ExitStack() as l3:
                        wq = self.sb(l3, [128, 8, 512], BF16, "wqk")
                        b_wq = Buf()
                        dsw = K.dsem(f"wqk{i}{blk}")
                        K.dma(K.pool, dsw, wq[:], self.W[("w_in", i)].ap()[:, 1536 + blk * 512:2048 + blk * 512].rearrange("(k p) c -> p k c", p=128),
                              [], [b_wq])
                        for gidx, (t0, n) in enumerate(groups):
                            w = n * 128
                            isctx = t0 >= NT
                            if isctx and blk == 0 and not ctx_upd:
                                continue
                            for hd in range(4):
                                s = cnt % 2
                                cnt += 1
                                pbk = s
                                for k in range(8):
                                    K.op(K.pe, lambda k=k, pbk=pbk, hd=hd: nc.tensor.matmul(
                                        self.pb[pbk][:, 0:w], lhsT=wq[:, k, hd * 128:(hd + 1) * 128],
                                        rhs=uT[:, k, t0 * 128:t0 * 128 + w], start=(k == 0), stop=(k == 7)),
                                        [b_wq] + b_uT[t0:t0 + n], [self.b_pb[pbk]], mark=(k == 7))
                                if isctx:
                                    dst = QT[:, hd, TOK:TOK + w] if blk == 0 else kc[:, hd, :]
                                    bd = b_QT if blk == 0 else b_kc
                                    K.op(K.act, lambda dst=dst, pbk=pbk: nc.scalar.copy(out=dst, in_=self.pb[pbk][:, 0:w]), [self.b_pb[pbk]], [bd])
                                    continue
                                if "v1" in DBG_SKIP:
                                    K.op(K.act, lambda s=s, pbk=pbk: nc.scalar.copy(out=qb[s][:, :], in_=self.pb[pbk][:, :]), [self.b_pb[pbk]], [b_qb[s]])
                                    continue
                                if "v6" in DBG_SKIP:
                                    K.op(K.act, lambda s=s, pbk=pbk: nc.scalar.activation(out=qb[s][:, :], in_=self.pb[pbk][:, :], func=AF.Identity), [self.b_pb[pbk]], [b_qb[s]])
                                    continue
                                if "v7" in DBG_SKIP:
                                    K.op(K.act, lambda s=s, pbk=pbk, hd=hd: nc.scalar.copy(out=stg[:, hd, :], in_=self.pb[pbk][:, :]), [self.b_pb[pbk]], [b_stg])
                                    continue
                                if "v8" in DBG_SKIP:
                                    K.op(K.act, lambda s=s, pbk=pbk: nc.scalar.copy(out=qb[0][:, :], in_=self.pb[pbk][:, :]), [self.b_pb[pbk]], [b_qb[0]])
                                    continue
                                if "v2" in DBG_SKIP:
                                    K.op(K.act, lambda s=s, pbk=pbk: nc.scalar.copy(out=qb[s][:, :], in_=self.pb[pbk][:, :]), [self.b_pb[pbk]], [b_qb[s]])
                                    K.op(K.pe, lambda s=s: nc.tensor.matmul(self.pb[2 + s][:, :], lhsT=self.cn["rotT"][:], rhs=qb[s][:, :], start=True, stop=True),
                                         [self.b_cn["rotT"], b_qb[s]], [self.b_pb[2 + s]], mark=True)
                                    continue
                                if "v3" in DBG_SKIP:
                                    K.op(K.dve, lambda s=s, pbk=pbk: nc.vector.tensor_tensor(out=r1[:], in0=self.pb[pbk][:, :], in1=cs[0][:, t0 * 128:t0 * 128 + 512], op=ALU.mult),
                                         [self.b_pb[pbk], b_cs], [b_r])
                                    continue
                                if "v4" in DBG_SKIP:
                                    K.op(K.dve, lambda s=s, pbk=pbk: nc.vector.tensor_copy(out=r1[:], in_=self.pb[pbk][:, :]),
                                         [self.b_pb[pbk]], [b_r])
                                    continue
                                K.op(K.dve, lambda s=s, pbk=pbk: nc.vector.tensor_copy(out=qb[s][:, :], in_=self.pb[pbk][:, :]), [self.b_pb[pbk]], [b_qb[s]])
                                K.op(K.pe, lambda s=s: nc.tensor.matmul(self.pb[2 + s][:, :], lhsT=self.cn["rotT"][:], rhs=qb[s][:, :], start=True, stop=True),
                                     [self.b_cn["rotT"], b_qb[s]], [self.b_pb[2 + s]], mark=True)
                                K.op(K.dve, lambda s=s, pbk=pbk: nc.vector.tensor_tensor(out=r1[:], in0=self.pb[pbk][:, :], in1=cs[0][:, t0 * 128:t0 * 128 + 512], op=ALU.mult),
                                     [self.b_pb[pbk], b_cs], [b_r])
                                K.op(K.dve, lambda s=s: nc.vector.tensor_tensor(out=self.pb[2 + s][:, :], in0=self.pb[2 + s][:, :], in1=cs[1][:, t0 * 128:t0 * 128 + 512], op=ALU.mult),
                                     [self.b_pb[2 + s], b_cs], [self.b_pb[2 + s]])
                                if blk == 0:
                                    K.op(K.dve, lambda s=s, hd=hd: nc.vector.tensor_tensor(out=QT[:, hd, t0 * 128:t0 * 128 + 512], in0=self.pb[2 + s][:, :], in1=r1[:], op=ALU.add),
                                         [b_r, self.b_pb[2 + s]], [b_QT])
                                else:
                                    K.op(K.dve, lambda s=s, hd=hd: nc.vector.tensor_tensor(out=stg[:, hd, :], in0=self.pb[2 + s][:, :], in1=r1[:], op=ALU.add),
                                         [b_r, self.b_pb[2 + s]], [b_stg])
                            if blk == 1 and not isctx:
                                K.dma(K.sync, dsx, self.xk_in.ap()[0:512, t0 * 128:t0 * 128 + 512].rearrange("(h p) c -> p h c", p=128), stg[:],
                                      [b_stg], [self.b_xin])
                        K.barrier()
            if ckpt("A3a"):
                return
            with ExitStack() as ls:
                wv = self.sb(ls, [128, 8, 512], BF16, "wv")
                b_wv = Buf()
                dsw = K.dsem(f"wv{i}")
                K.dma(K.pool, dsw, wv[:], self.W[("w_in", i)].ap()[:, 2560:3072].rearrange("(k p) c -> p k c", p=128), [], [b_wv])
                stg = [self.sb(ls, [128, 4, 512], BF16, "stgv") for _ in range(2)]
                b_stg = [Buf(), Buf()]
                for gidx, (t0, n) in enumerate(groups):
                    isctx = t0 >= NT
                    ss = gidx % 2
                    for tt in range(n):
                        pbk = 4 + (tt % 2)
                        for k in range(8):
                            K.op(K.pe, lambda k=k, pbk=pbk, tt=tt: nc.tensor.matmul(
                                self.pb[pbk][:, :], lhsT=uT[:, k, (t0 + tt) * 128:(t0 + tt + 1) * 128], rhs=wv[:, k, :],
                                start=(k == 0), stop=(k == 7)), [b_wv, b_uT[t0 + tt]], [self.b_pb[pbk]], mark=(k == 7))
                        if isctx:
                            K.op(K.act, lambda tt=tt, pbk=pbk: nc.scalar.copy(out=vc[:, tt, :], in_=self.pb[pbk][:, :]), [self.b_pb[pbk]], [b_vc])
                        else:
                            K.op(K.act, lambda tt=tt, pbk=pbk, ss=ss: nc.scalar.copy(out=stg[ss][:, tt, :], in_=self.pb[pbk][:, :]), [self.b_pb[pbk]], [b_stg[ss]])
                    if not isctx:
                        vdst = self.xv_in.ap()[0:1, 0:1]
                        dst = bass.AP(vdst.tensor, vdst.offset + t0 * 128 * 512, [[512, 128], [128 * 512, 4], [1, 512]])
                        K.dma(K.sync, dsx, dst, stg[ss][:], [b_stg[ss]], [self.b_xin])
                K.barrier()
        if ckpt("A3"):
            return
        self.collective(self.xk_in.ap(), self.xk_out.ap(), self.b_xin, self.b_xout)
        self.collective(self.xv_in.ap(), self.xv_out.ap(), self.b_xin, self.b_xout)
        self.collective(self.xh_in.ap(), self.xh_out.ap(), self.b_xin, self.b_xout)
        if ckpt("xchg"):
            return
        with ExitStack() as ls:
            KT = self.sb(ls, [128, 2, NKT * 128], BF16, "KT")
            b_KT = Buf()
            VA = self.sb(ls, [128, NKT, 2, 130], BF16, "VA")
            b_VA = Buf()
            dsa = K.dsem(f"attl{i}")
            K.op(K.pool, lambda: nc.gpsimd.memset(VA[:, :, :, 128:130], 1.0), [], [b_VA])
            xko = self.xk_out.ap()
            xvo = self.xv_out.ap()
            PT = [self.sb(ls, [128, 512], BF16, "PT") for _ in range(3)]
            b_PT = [Buf() for _ in range(3)]
            osb = self.sb(ls, [128, 2, 130], F32, "osb")
            b_osb = Buf()
            o1 = self.sb(ls, [128, 128], F32, "o1")
            o2 = self.sb(ls, [128, 128], F32, "o2")
            ob = self.sb(ls, [128, 128], BF16, "ob")
            sq = self.sb(ls, [128, 128], F32, "sq")
            sc_ = self.sb(ls, [128, 8], F32, "sc_")
            b_o = Buf()
            qgroups = [(g * 512, 512, list(range(NKT))) for g in range(4)]
            if ctx_upd:
                qgroups.append((TOK, 256, [32, 33]))
            pcnt = 0
            Qm = [self.sb(ls, [128, 2, 512], BF16, "Qm") for _ in range(2)]
            b_Qm = [Buf(), Buf()]
            for k in range(2):
                K.op(K.pool, lambda k=k: nc.gpsimd.memset(Qm[k][:], 0.0), [], [b_Qm[k]])
            qcnt = 0
            Sb = [self.pb[0][:, :], self.pb[1][:, :], self.pbb[1][:].bitcast(F32)]
            b_Sb = [self.b_pb[0], self.b_pb[1], self.b_pbb[1]]
            for hp in range(2):
                for r in range(2):
                    K.dma(K.sync, dsa, KT[:, :, r * TOK:(r + 1) * TOK],
                          xko[r * 512 + hp * 256:r * 512 + hp * 256 + 256, :].rearrange("(h p) c -> p h c", p=128), [self.b_xout], [b_KT])
                    for h2 in range(2):
                        hd = hp * 2 + h2
                        base = xvo[r * 512:r * 512 + 1, 0:1]
                        src = bass.AP(base.tensor, base.offset + hd * 128, [[512, 128], [128 * 512, 16], [1, 128]])
                        K.dma(K.sync, dsa, VA[:, r * 16:(r + 1) * 16, h2, 0:128], src, [self.b_xout], [b_VA])
                K.op(K.act, lambda hp=hp: nc.scalar.copy(out=KT[:, :, 2 * TOK:2 * TOK + 256], in_=kc[:, hp * 2:hp * 2 + 2, :]), [b_kc], [b_KT])
                for tt in range(2):
                    K.op(K.act, lambda tt=tt, hp=hp: nc.scalar.copy(out=VA[:, 32 + tt, :, 0:128],
                                                                   in_=vc[:, tt, hp * 256:(hp + 1) * 256].rearrange("p (h d) -> p h d", h=2)),
                         [b_vc], [b_VA])
                heads = [(q0, qw, kts, h2) for (q0, qw, kts) in qgroups for h2 in range(2)]
                qbase = qcnt

                def emit_qm(k):
                    q0_, qw_, _, h2_ = heads[k]
                    qs_ = (qbase + k) % 2
                    for c in range(2):
                        K.op(K.pool, lambda c=c, qs_=qs_, q0_=q0_, qw_=qw_, h2_=h2_: nc.gpsimd.tensor_copy(
                            out=Qm[qs_][c * 64:(c + 1) * 64, c, 0:qw_], in_=QT[c * 64:(c + 1) * 64, hp * 2 + h2_, q0_:q0_ + qw_]),
                            [b_QT], [b_Qm[qs_]])

                emit_qm(0)
                qcnt += len(heads)
                for hk, (q0, qw, kts, h2) in enumerate(heads):
                    if True:
                        nqt = qw // 128
                        hd = hp * 2 + h2
                        qs = (qbase + hk) % 2
                        its = [(ki, kt, c) for ki, kt in enumerate(kts) for c in range(2)]

                        def emit_s(n):
                            ki, kt, c = its[n]
                            sp = (pbase + n) % 3
                            K.op(K.pe, lambda c=c, kt=kt, sp=sp: nc.tensor.matmul(
                                Sb[sp][:, 0:qw], lhsT=KT[:, h2, kt * 128:(kt + 1) * 128],
                                rhs=Qm[qs][:, c, 0:qw], start=True, stop=True),
                                [b_KT, b_Qm[qs]], [b_Sb[sp]], mark=True)

                        pbase = pcnt
                        emit_s(0)
                        if len(its) > 1:
                            emit_s(1)
                        for n, (ki, kt, c) in enumerate(its):
                            sp = (pbase + n) % 3
                            pp = (pbase + n) % 3
                            if n + 2 < len(its):
                                emit_s(n + 2)
                            K.op(K.act, lambda sp=sp, pp=pp: nc.scalar.activation(out=PT[pp][:, 0:qw], in_=Sb[sp][:, 0:qw], func=AF.Exp, scale=0.125),
                                 [b_Sb[sp]], [b_PT[pp]])
                            for qt in range(nqt):
                                K.op(K.pe, lambda c=c, kt=kt, qt=qt, pp=pp, ki=ki: nc.tensor.matmul(
                                    self.pb[2 + qt][:, c * 256:c * 256 + 129], lhsT=PT[pp][:, qt * 128:(qt + 1) * 128],
                                    rhs=VA[:, kt, h2, 0:129], start=(ki == 0 and c == 0), stop=(ki == len(kts) - 1),
                                    skip_group_check=True),
                                    [b_PT[pp], b_VA], [self.b_pb[2 + qt]], mark=(ki == len(kts) - 1 and c == 1))
                        pcnt += len(its)
                        if hk + 1 < len(heads):
                            emit_qm(hk + 1)
                        for qt in range(nqt):
                            bk = 2 + qt
                            K.op(K.act, lambda bk=bk: nc.scalar.copy(out=osb[:, :, 0:129],
                                                                      in_=self.pb[bk][:].rearrange("p (c x) -> p c x", c=2)[:, :, 0:129]),
                                 [self.b_pb[bk]], [b_osb])
                            K.op(K.dve, lambda: nc.vector.reciprocal(out=sc_[:, 0:2], in_=osb[:, :, 128]), [b_osb], [b_o])
                            K.op(K.dve, lambda: nc.vector.tensor_tensor(out=sc_[:, 2:3], in0=sc_[:, 1:2], in1=small[:, 4:5], op=ALU.mult), [b_o, b_small], [b_o])
                            K.op(K.dve, lambda: nc.vector.tensor_scalar(out=o1[:], in0=osb[:, 0, 0:128], scalar1=sc_[:, 0:1], scalar2=None, op0=ALU.mult),
                                 [b_osb, b_o], [b_o])
                            K.op(K.dve, lambda: nc.vector.scalar_tensor_tensor(out=o2[:], in0=osb[:, 1, 0:128], scalar=sc_[:, 2:3], in1=o1[:],
                                                                                op0=ALU.mult, op1=ALU.add), [b_osb, b_o], [b_o])
                            K.op(K.dve, lambda: nc.vector.tensor_tensor(out=sq[:], in0=o2[:], in1=o2[:], op=ALU.mult), [b_o], [b_o])
                            K.op(K.dve, lambda: nc.vector.tensor_reduce(out=sc_[:, 3:4], in_=sq[:], op=ALU.add, axis=mybir.AxisListType.X), [b_o], [b_o])
                            K.op(K.pool, lambda: nc.gpsimd.tensor_scalar(out=sc_[:, 4:5], in0=sc_[:, 3:4], scalar1=1.0 / 128.0, scalar2=SUBLN_EPS, op0=ALU.mult, op1=ALU.add),
                                 [b_o], [b_o])
                            K.op(K.pool, lambda: nc.gpsimd.tensor_tensor(out=sc_[:, 5:6], in0=sc_[:, 4:5], in1=self.eps_t[:, 2:3], op=ALU.pow),
                                 [b_o, self.b_eps], [b_o])
                            K.op(K.dve, lambda: nc.vector.scalar_tensor_tensor(out=ob[:], in0=o2[:], scalar=sc_[:, 5:6], in1=sg[:],
                                                                                op0=ALU.mult, op1=ALU.mult), [b_o, b_sg], [b_o])
                            K.op(K.pe, lambda: nc.tensor.transpose(out=self.pbb[0][:, 0:128], in_=ob[:], identity=self.cn["ident_bf"][:]),
                                 [b_o, self.b_cn["ident_bf"]], [self.b_pbb[0]], mark=True)
                            K.op(K.dve, lambda hd=hd, qt=qt: nc.vector.tensor_copy(out=slT[:, hd, q0 + qt * 128:q0 + (qt + 1) * 128], in_=self.pbb[0][:, 0:128]),
                                 [self.b_pbb[0]], [b_sl])
            K.barrier()
        if ckpt("attn"):
            return
        with ExitStack() as ls:
            lnb = self.ln_bc(ls, i, 1)
            wo = self.sb(ls, [128, 8, D], BF16, "wo")
            b_wo = Buf()
            dsw = K.dsem(f"wo{i}")
            K.dma(K.pool, dsw, wo[:], self.W[("w_out_mix", i)].ap().rearrange("(k p) c -> p k c", p=128), [], [b_wo])
            cmT = self.sb(ls, [128, 4, TOKA], BF16, "cmT")
            b_cmT = Buf()
            dsw = K.dsem(f"cmr{i}")
            K.dma(K.sync, dsw, cmT[:], self.cmd.ap().rearrange("p (c t) -> p c t", c=4), [self.b_cmd], [b_cmT])
            hl = self.sb(ls, [128, 4, 2], BF16, "hl")
            b_hl = Buf()
            hb = self.xh_out.ap()[0:1, 0:1]
            K.dma(K.sync, dsw, hl[:, :, 0:1], bass.AP(hb.tensor, hb.offset + 512, [[1, 128], [128, 4], [1, 1]]), [self.b_xout], [b_hl])
            K.dma(K.sync, dsw, hl[:, :, 1:2], bass.AP(hb.tensor, hb.offset + TOK, [[1, 128], [128, 4], [1, 1]]), [self.b_xout], [b_hl])
            hf_ = self.sb(ls, [128, 4, 2], F32, "hf")
            b_hf = Buf()
            for c in range(4):
                for wi, (tap, colv) in enumerate(((0, 0), (2, TOK - 1))):
                    K.op(K.dve, lambda c=c, wi=wi, tap=tap: nc.vector.tensor_scalar(out=hf_[:, c, wi:wi + 1], in0=hl[:, c, wi:wi + 1],
                                                                                     scalar1=cw[:, c, tap:tap + 1], scalar2=self.cn["halo_mask"][:, wi:wi + 1],
                                                                                     op0=ALU.mult, op1=ALU.mult), [b_hl, b_cw, self.b_cn["halo_mask"]], [b_hf])
                    K.op(K.dve, lambda c=c, wi=wi, colv=colv: nc.vector.scalar_tensor_tensor(out=cmT[:, c, colv:colv + 1], in0=hf_[:, c, wi:wi + 1],
                                                                                              scalar=gbh[:, c, wi:wi + 1], in1=cmT[:, c, colv:colv + 1],
                                                                                              op0=ALU.mult, op1=ALU.add), [b_hf, b_gbh, b_cmT], [b_cmT])
            for t in range(ntu):
                if True:
                    l2 = ls
                    pk = [2 + 2 * (t % 2), 3 + 2 * (t % 2)]
                    for hf in range(2):
                        for k in range(8):
                            if k < 4:
                                lhs = cmT[:, k, t * 128:(t + 1) * 128]
                                rb = b_cmT
                            else:
                                lhs = slT[:, k - 4, t * 128:(t + 1) * 128]
                                rb = b_sl
                            K.op(K.pe, lambda k=k, hf=hf, lhs=lhs: nc.tensor.matmul(
                                self.pb[pk[hf]][:, :], lhsT=lhs, rhs=wo[:, k, hf * 512:(hf + 1) * 512], start=(k == 0), stop=(k == 7)),
                                [rb, b_wo], [self.b_pb[pk[hf]]], mark=(k == 7))
                    self.residual_ln(l2, t, [self.pb[pk[0]][:, :], self.pb[pk[1]][:, :]], [self.b_pb[pk[0]], self.b_pb[pk[1]]],
                                     bc, "c" if t >= NT else "", lnb)
            K.barrier()

    def odd_mixer(self, st, i, j, ctx_upd, ntl, ntu):
        nc, K = self.nc, self.K
        ntl = ntu
        ncols = ntu * 128
        uc = self.sb(st, [128, 2, D], BF16, "uc")
        b_uc = Buf()
        with ExitStack() as ls:
            bc = self.alloc_mods(st, ls, ctx_upd, bc_all=True)
            self.mods(i, 1, bc)
            ub = [self.sb(ls, [128, D], BF16, "ub") for _ in range(2)]
            b_ub = [Buf(), Buf()]
            dsu = K.dsem(f"fu{i}")
            for t in range(ntu):
                if True:
                    l2 = ls
                    if t >= NT:
                        self.modulate_tok(l2, t, bc, "c", uc[:, t - NT, :], b_uc)
                    else:
                        s = t % 2
                        self.modulate_tok(l2, t, bc, "", ub[s][:], b_ub[s])
                        K.dma(K.sync, dsu, self.fin[t // 8].ap()[(t % 8) * 128:(t % 8 + 1) * 128, :], ub[s][:], [b_ub[s]], [self.b_fin])
            K.barrier()
        for k in range(2):
            self.collective(self.fin[k].ap(), self.fout[k].ap(), self.b_fin, self.b_fout)
        with ExitStack() as ls:
            X = [self.sb(ls, [64, 8 * D], BF16, "X") for _ in range(2)]
            b_X = [Buf(), Buf()]
            Y = [self.sb(ls, [128, 8 * D], BF16, "Y") for _ in range(2)]
            b_Y = [Buf(), Buf()]
            dsX = [K.dsem(f"fX{i}{k}") for k in range(2)]
            dsY = K.dsem(f"fY{i}")
            for nb in range(8):
                s = nb % 2
                for rk in range(2):
                    for jj in range(2):
                        fo = self.fout[jj].ap()
                        base = fo[rk * 1024 + nb * 8:rk * 1024 + nb * 8 + 1, 0:1]
                        src = bass.AP(base.tensor, base.offset, [[64 * D, 16], [1, 8 * D]])
                        p0 = rk * 32 + jj * 16
                        K.dma(K.sync, dsX[s], X[s][p0:p0 + 16, :], src, [self.b_fout], [b_X[s]])
                for cc in range(16):
                    pbk = cc % 4
                    K.op(K.pe, lambda s=s, cc=cc, pbk=pbk: nc.tensor.matmul(
                        self.pb[pbk][:, :], lhsT=self.cn["t1"][:, :], rhs=X[s][:, cc * 512:(cc + 1) * 512], start=True, stop=True),
                        [self.b_cn["t1"], b_X[s]], [self.b_pb[pbk]], mark=True)
                    eng = K.act if cc % 2 == 0 else K.dve
                    if cc % 2 == 0:
                        K.op(K.act, lambda s=s, cc=cc, pbk=pbk: nc.scalar.copy(out=Y[s][:, cc * 512:(cc + 1) * 512], in_=self.pb[pbk][:, :]),
                             [self.b_pb[pbk]], [b_Y[s]])
                    else:
                        K.op(K.dve, lambda s=s, cc=cc, pbk=pbk: nc.vector.tensor_copy(out=Y[s][:, cc * 512:(cc + 1) * 512], in_=self.pb[pbk][:, :]),
                             [self.b_pb[pbk]], [b_Y[s]])
                K.dma(K.sync, dsY, self.yd.ap()[:, nb * 8 * D:(nb + 1) * 8 * D], Y[s][:], [b_Y[s]], [self.b_yd])
            K.barrier()
        def f4(PQ, b_PQ, groups, colbase):
            with ExitStack() as ls:
                lnb = self.ln_bc(ls, i, 1)
                wo = self.sb(ls, [128, 8, D], BF16, "wof")
                b_wo = Buf()
                dsw = K.dsem(f"fw{i}{colbase}")
                K.dma(K.pool, dsw, wo[:], self.W[("w_out_fourier", i)].ap().rearrange("(k p) c -> p k c", p=128), [], [b_wo])
                fT = self.sb(ls, [128, 8, 512], BF16, "fT")
                b_fT = Buf()
                ccs = self.cn["ccs"]
                for gi, (t0, n) in enumerate(groups):
                    w = n * 128
                    cb = t0 * 128 - colbase
                    for g in range(8):
                        pbk = g % 2
                        for pq in range(2):
                            K.op(K.pe, lambda g=g, pq=pq, pbk=pbk: nc.tensor.matmul(
                                self.pb[pbk][:, 0:w], lhsT=ccs[:, pq, :], rhs=PQ[:, pq, g, cb:cb + w], start=(pq == 0), stop=(pq == 1)),
                                [self.b_cn["ccs"], b_PQ], [self.b_pb[pbk]], mark=(pq == 1))
                        K.op(K.act, lambda g=g, pbk=pbk: nc.scalar.copy(out=fT[:, g, 0:w], in_=self.pb[pbk][:, 0:w]), [self.b_pb[pbk]], [b_fT])
                    for tt in range(n):
                        t = t0 + tt
                        pk = [2 + 2 * (tt % 2), 3 + 2 * (tt % 2)]
                        for hf in range(2):
                            for k in range(8):
                                K.op(K.pe, lambda k=k, hf=hf, tt=tt: nc.tensor.matmul(
                                    self.pb[pk[hf]][:, :], lhsT=fT[:, k, tt * 128:(tt + 1) * 128], rhs=wo[:, k, hf * 512:(hf + 1) * 512],
                                    start=(k == 0), stop=(k == 7)), [b_fT, b_wo], [self.b_pb[pk[hf]]], mark=(k == 7))
                        self.residual_ln(ls, t, [self.pb[pk[0]][:, :], self.pb[pk[1]][:, :]], [self.b_pb[pk[0]], self.b_pb[pk[1]]],
                                         bc_g, "c" if t >= NT else "", lnb)
                K.barrier()

        bc_g = bc
        with ExitStack() as lp:
            PQ = self.sb(lp, [128, 2, 8, TOK], BF16, "PQ")
            b_PQ = Buf()
            with ExitStack() as ls:
                G = self.sb(ls, [128, 64, 64], BF16, "G")
                b_G = Buf()
                dsg = K.dsem(f"fG{i}")
                K.dma(K.sync, dsg, G[:], self.Cd["g2"].ap(), [], [b_G])
                Yk = [self.sb(ls, [128, 4, D], BF16, "Yk") for _ in range(2)]
                b_Yk = [Buf(), Buf()]
                dsK = [K.dsem(f"fK{i}{k}") for k in range(2)]
                yda = self.yd.ap()
                for kb in range(16):
                    s = kb % 2
                    for ri in range(2):
                        base = yda[ri * 64 + kb * 4:ri * 64 + kb * 4 + 1, 0:1]
                        src_ = bass.AP(base.tensor, base.offset, [[D, 64], [64 * D, 4], [1, D]])
                        K.dma(K.sync, dsK[s], Yk[s][ri * 64:(ri + 1) * 64, :, :], src_, [self.b_yd], [b_Yk[s]])
                    for g in range(8):
                        pbk = g % 4
                        for kk in range(4):
                            K.op(K.pe, lambda s=s, g=g, kk=kk, pbk=pbk, kb=kb: nc.tensor.matmul(
                                self.pb[pbk][:, kk * 64:(kk + 1) * 64], lhsT=Yk[s][:, kk, g * 128:(g + 1) * 128], rhs=G[:, kb * 4 + kk, :],
                                start=True, stop=True, skip_group_check=True),
                                [b_Yk[s], b_G], [self.b_pb[pbk]], mark=(kk == 3))
                        for pq in range(2):
                            dst = PQ[:, pq, g, 0:TOK].rearrange("p (b a) -> p a b", a=64)[:, kb * 4:(kb + 1) * 4, :]
                            srcp = self.pb[pbk][:, 0:256].rearrange("p (k q b) -> p k q b", k=4, q=2)[:, :, pq, :]
                            if pq == 0:
                                K.op(K.act, lambda dst=dst, srcp=srcp: nc.scalar.copy(out=dst, in_=srcp), [self.b_pb[pbk]], [b_PQ])
                            else:
                                K.op(K.dve, lambda dst=dst, srcp=srcp: nc.vector.tensor_copy(out=dst, in_=srcp), [self.b_pb[pbk]], [b_PQ])
                K.barrier()
            f4(PQ, b_PQ, [(g * 4, 4) for g in range(4)], 0)
        if ctx_upd:
            with ExitStack() as lp:
                PQc = self.sb(lp, [128, 2, 8, 256], BF16, "PQc")
                b_PQc = Buf()
                with ExitStack() as ls:
                    c256 = self.sb(ls, [128, 2, 2, 256], BF16, "c256")
                    b_c256 = Buf()
                    dsc = K.dsem(f"fc{i}")
                    K.dma(K.sync, dsc, c256[:], self.Cd["c256"].ap(), [], [b_c256])
                    for g in range(8):
                        pbk = g % 4
                        for tt in range(2):
                            K.op(K.pe, lambda g=g, tt=tt, pbk=pbk: nc.tensor.matmul(
                                self.pb[pbk][:, :], lhsT=uc[:, tt, g * 128:(g + 1) * 128], rhs=c256[:, tt, :, :].rearrange("p a b -> p (a b)"),
                                start=(tt == 0), stop=(tt == 1)), [b_uc, b_c256], [self.b_pb[pbk]], mark=(tt == 1))
                        K.op(K.act, lambda g=g, pbk=pbk: nc.scalar.copy(out=PQc[:, :, g, :],
                                                                        in_=self.pb[pbk][:].rearrange("p (a b) -> p a b", a=2)), [self.b_pb[pbk]], [b_PQc])
                    K.barrier()
                f4(PQc, b_PQc, [(16, 2)], TOK)


_PROGS = {}
LAYER_GROUPS = [[0, 1, 2, 3]]


def _get_prog(layers):
    key = tuple(layers)
    if key not in _PROGS:
        _PROGS[key] = Prog(list(layers))
    return _PROGS[key]


def run_layers(inputs, layer_groups, h=None, hc=None):
    x = np.asarray(inputs["x"], np.float32)
    B = x.shape[0]
    if h is None:
        h = x
        hc = np.asarray(inputs["ctx"], np.float32)
    consts = [make_consts(0), make_consts(1)]
    full = {n: np.asarray(inputs[n], np.float32) for n in W_SPECS}
    c = np.asarray(inputs["c"], np.float32)
    c_ctx = np.asarray(inputs["c_ctx"], np.float32)
    for layers in layer_groups:
        prog = _get_prog(layers)
        in_maps = []
        for r in range(8):
            b, half = r // 2, r % 2
            m = {"x_in": np.ascontiguousarray(h[b, half * TOK:(half + 1) * TOK]),
                 "hc_in": np.ascontiguousarray(hc[b]),
                 "cvec": np.ascontiguousarray(np.stack([c[b], c_ctx], 0))}
            for (n, i), tns in prog.W.items():
                per = 2 if n in ("w_in", "conv_w", "lambda_qk", "subln_g", "w_out_mix", "w_out_fourier") else 1
                m[tns.name] = np.ascontiguousarray(full[n][i // per])
            m.update(consts[half])
            in_maps.append(m)
        res = run_bass_kernel_spmd(prog.nc, in_maps, core_ids=list(range(8)))
        h = np.stack([np.concatenate([res.results[2 * b]["h_out"], res.results[2 * b + 1]["h_out"]], 0) for b in range(B)], 0)
        hc = np.stack([res.results[2 * b]["hc_out"] for b in range(B)], 0)
    return h, hc


def kernel(**inputs):
    h, _ = run_layers(inputs, LAYER_GROUPS)
    return np.ascontiguousarray(h.astype(np.float32))
```

```python
import math
import bisect
from contextlib import ExitStack
import numpy as np
import ml_dtypes
import concourse.bass as bass
import concourse.mybir as mybir
from concourse.bass_utils import run_bass_kernel_spmd

F32 = mybir.dt.float32
BF16 = mybir.dt.bfloat16
AF = mybir.ActivationFunctionType
ALU = mybir.AluOpType

D = 1024
NT = 16
NTC = 2
TOK = 2048
TOKA = 2304
DEPTH = 4
ALPHA = (2.0 * DEPTH) ** 0.25
LN_EPS = 1e-6
SUBLN_EPS = 1e-5
NKT = 34
DEBUG_ALLOC = False
SBUF_BUDGET = 192 * 1024 - 64
STOP_AFTER = None
DBG_SKIP = set()


class StopBuild(Exception):
    pass


STOPPED = [False]


def ckpt(name):
    if STOP_AFTER == name:
        STOPPED[0] = True
    return STOPPED[0]
XROWS = 1025


class Eng:
    def __init__(self, name, h, sem, is_pe=False):
        self.name, self.h, self.sem = name, h, sem
        self.insts = []
        self.marked = []
        self.markval = {}
        self.waited = {}
        self.is_pe = is_pe


class DSem:
    def __init__(self, name, ctx):
        self.name, self.ctx, self.slot = name, ctx, None

    def bind(self, kind):
        if self.slot is None:
            self.slot = self.ctx.get_slot(kind)
            self.ctx.live.append(self)

    @property
    def sem(self):
        return self.slot[0]

    @property
    def cnt(self):
        return self.slot[1]

    @cnt.setter
    def cnt(self, v):
        self.slot[1] = v


class Buf:
    __slots__ = ("w", "r", "name", "excl")

    def __init__(self, name="", excl=False):
        self.w = None
        self.r = {}
        self.name = name
        self.excl = excl


class Ctx:
    def __init__(self, nc, es):
        self.nc, self.es = nc, es
        mk = lambda n: es.enter_context(nc.semaphore(n))
        self.pe = Eng("pe", nc.tensor, mk("s_pe"), True)
        self.act = Eng("act", nc.scalar, mk("s_act"))
        self.dve = Eng("dve", nc.vector, mk("s_dve"))
        self.pool = Eng("pool", nc.gpsimd, mk("s_pool"))
        self.sync = Eng("sync", nc.sync, mk("s_sync"))
        self.engs = [self.pe, self.act, self.dve, self.pool, self.sync]
        self.dsems = []
        self.nsem = 5

    def dsem(self, name):
        return DSem(name, self)

    def get_slot(self, kind):
        if not hasattr(self, "free"):
            self.free = {"sw": [], "hw": [], "cc": []}
            self.live = []
            self.slots = []
        if self.free[kind]:
            return self.free[kind].pop()
        self.nsem += 1
        slot = [self.es.enter_context(self.nc.semaphore(f"d_{kind}{self.nsem}")), 0, kind]
        self.slots.append(slot)
        return slot

    def recycle(self):
        for d in self.live:
            if d.slot[2] != "cc":
                self.free[d.slot[2]].append(d.slot)
        self.live = [d for d in self.live if d.slot[2] == "cc"]

    def _wait_ev(self, eng, ev):
        src, idx = ev
        if isinstance(src, DSem):
            val = src.cnt
            if eng.waited.get(id(src.slot), 0) >= val:
                return
            eng.h.wait_ge(src.sem, val)
            eng.waited[id(src.slot)] = val
            return
        if src is eng and (eng.is_pe or eng is self.sync):
            return
        if eng.waited.get(id(src), -1) >= idx:
            return
        k = bisect.bisect_left(src.marked, idx)
        if k < len(src.marked):
            j = src.marked[k]
        else:
            j = len(src.insts) - 1
            self._mark(src, j)
        eng.h.wait_ge(src.sem, src.markval[j])
        eng.waited[id(src)] = j

    def _mark(self, src, j):
        assert j == len(src.insts) - 1 and j not in src.markval
        src.insts[j].then_inc(src.sem, 1)
        src.marked.append(j)
        src.markval[j] = len(src.marked)

    def _deps(self, eng, reads, writes):
        evs = []
        for b in reads:
            if b.w is not None:
                evs.append(b.w)
            if b.excl:
                for s, (src, idx) in b.r.items():
                    if src is not eng:
                        evs.append((src, idx))
        for b in writes:
            if b.w is not None:
                evs.append(b.w)
            for s, (src, idx) in b.r.items():
                evs.append((src, idx))
        for ev in evs:
            self._wait_ev(eng, ev)

    def _record(self, ev, reads, writes):
        for b in reads:
            src, idx = ev
            key = id(src.slot) if isinstance(src, DSem) else id(src)
            old = b.r.get(key)
            if old is None or old[1] < idx:
                b.r[key] = ev
        for b in writes:
            b.w = ev
            b.r = {}

    def op(self, eng, fn, reads=(), writes=(), mark=None):
        self._deps(eng, reads, writes)
        inst = fn()
        eng.insts.append(inst)
        j = len(eng.insts) - 1
        if mark is None:
            mark = not eng.is_pe
        if mark:
            self._mark(eng, j)
        self._record((eng, j), reads, writes)
        return inst

    def dma(self, q, ds, out, in_, reads=(), writes=()):
        self._deps(q, reads, writes)
        ds.bind("sw" if q is self.pool else "hw")
        inst = q.h.dma_start(out=out, in_=in_)
        ds.cnt += 16
        inst.then_inc(ds.sem, 16)
        self._record((ds, ds.cnt), reads, writes)

    def barrier(self):
        for e in self.engs:
            for s in self.engs:
                if s is e or not s.insts:
                    continue
                self._wait_ev(e, (s, len(s.insts) - 1))
            for d in getattr(self, "live", []):
                if d.cnt:
                    self._wait_ev(e, (d, d.cnt))


def _bf(a):
    return np.ascontiguousarray(a.astype(ml_dtypes.bfloat16))


def make_consts(half):
    c = {}
    c["ident_bf"] = _bf(np.eye(128, dtype=np.float32))
    c["ident_f"] = np.eye(128, dtype=np.float32)
    Rm = np.zeros((128, 128), np.float32)
    for blk in range(4):
        o = blk * 32
        for d in range(16):
            Rm[o + d, o + d + 16] = -1.0
            Rm[o + d + 16, o + d] = 1.0
    c["rotT"] = _bf(Rm.T)
    t = np.arange(TOK) + half * TOK
    row, col = t // 64, t % 64
    inv = 10000.0 ** (-np.arange(16, dtype=np.float32) / 16)
    ang = np.zeros((128, TOK), np.float32)
    for p in range(128):
        d = p % 64
        pos = row if d < 32 else col
        ang[p] = pos.astype(np.float32) * inv[d % 16]
    c["cos"] = _bf(np.cos(ang))
    c["sin"] = _bf(np.sin(ang))
    n2 = np.arange(64)[:, None].astype(np.float64)
    k1 = np.arange(64)[None, :].astype(np.float64)
    a = 2 * np.pi * n2 * k1 / 64
    c["t1"] = _bf(np.concatenate([np.cos(a), np.sin(a)], 1).astype(np.float32) / 8.0)
    n1 = np.arange(64).astype(np.float64)
    G = np.zeros((128, 64, 64), np.float32)
    for kk1 in range(64):
        k = kk1 + 64 * (np.arange(32) + 32 * half)
        a = 2 * np.pi * n1[:, None] * k[None, :] / 4096.0
        gc, gs = np.cos(a) / 8.0, np.sin(a) / 8.0
        G[0:64, kk1, 0:32] = gc
        G[64:128, kk1, 0:32] = -gs
        G[0:64, kk1, 32:64] = gs
        G[64:128, kk1, 32:64] = gc
    c["g2"] = _bf(G)
    ch = np.arange(128).astype(np.float64)
    a = 2 * np.pi * ch[:, None] * ch[None, :] / 128.0
    cs = np.stack([np.cos(a), -np.sin(a)], 1) / math.sqrt(128.0)
    c["ccs"] = _bf(cs.astype(np.float32))
    n = np.arange(256).astype(np.float64)
    a = 2 * np.pi * n[:, None] * n[None, :] / 256.0
    m = np.stack([np.cos(a), np.sin(a)], 1) / 16.0
    c["c256"] = _bf(m.reshape(2, 128, 2, 256).transpose(1, 0, 2, 3).astype(np.float32))
    c["halo_mask"] = np.tile(np.array([[float(half), float(1 - half)]], np.float32), (128, 1))
    c["ones_f"] = np.ones((64, 128), np.float32)
    return c


CONST_SPECS = {
    "ident_bf": ([128, 128], BF16), "ident_f": ([128, 128], F32), "rotT": ([128, 128], BF16),
    "cos": ([128, TOK], BF16), "sin": ([128, TOK], BF16), "t1": ([64, 128], BF16),
    "g2": ([128, 64, 64], BF16), "ccs": ([128, 2, 128], BF16), "c256": ([128, 2, 2, 256], BF16),
    "halo_mask": ([128, 2], F32), "ones_f": ([64, 128], F32),
}

W_SPECS = {
    "ada_w": [DEPTH, D, 6 * D], "ada_b": [DEPTH, 6 * D], "ln_g": [DEPTH, 2, D], "ln_b": [DEPTH, 2, D],
    "mlp_w1": [DEPTH, D, 4 * D], "mlp_w2": [DEPTH, 4 * D, D], "w_in": [2, D, 3072], "conv_w": [2, 3, 512],
    "lambda_qk": [2, 4, 64], "subln_g": [2, 128], "w_out_mix": [2, D, D], "w_out_fourier": [2, D, D],
}


class Prog:
    def __init__(self, layers):
        self.layers = layers
        nc = self.nc = bass.Bass("TRN2", target_bir_lowering=False)
        self.es = ExitStack()
        self.K = Ctx(nc, self.es)
        K = self.K
        dt = lambda n, s, d, k: nc.dram_tensor(n, s, d, kind=k)
        self.x_in = dt("x_in", [TOK, D], F32, "ExternalInput")
        self.hc_in = dt("hc_in", [256, D], F32, "ExternalInput")
        self.cvec = dt("cvec", [2, D], F32, "ExternalInput")
        self.h_out = dt("h_out", [TOK, D], F32, "ExternalOutput")
        self.hc_out = dt("hc_out", [256, D], F32, "ExternalOutput")
        self.W = {}
        for n, s in W_SPECS.items():
            per = 2 if n in ("w_in", "conv_w", "lambda_qk", "subln_g", "w_out_mix", "w_out_fourier") else 1
            for i in self.layers:
                idx = i // per
                if n == "w_out_fourier" and i % 2 == 0:
                    continue
                if n in ("w_in", "conv_w", "lambda_qk", "subln_g", "w_out_mix") and i % 2 == 1:
                    continue
                self.W[(n, i)] = dt(f"{n}_{idx}", s[1:], F32, "ExternalInput")
        self.Cd = {n: dt(n, s, d, "ExternalInput") for n, (s, d) in CONST_SPECS.items()}
        self.xk_in = dt("xk_in", [512, TOK], BF16, "Internal")
        self.xk_out = dt("xk_out", [1024, TOK], BF16, "Internal")
        self.xv_in = dt("xv_in", [512, TOK], BF16, "Internal")
        self.xv_out = dt("xv_out", [1024, TOK], BF16, "Internal")
        self.xh_in = dt("xh_in", [1, TOK], BF16, "Internal")
        self.xh_out = dt("xh_out", [2, TOK], BF16, "Internal")
        self.fin = [dt(f"fin{k}", [1024, D], BF16, "Internal") for k in range(2)]
        self.fout = [dt(f"fout{k}", [2048, D], BF16, "Internal") for k in range(2)]
        self.yd = dt("yd", [128, 64 * D], BF16, "Internal")
        self.cmd = dt("cmd", [128, 4 * TOKA], BF16, "Internal")
        self.b_cmd = Buf("cmd")
        self.b_xin, self.b_xout, self.b_fin, self.b_fout, self.b_yd = (Buf(n) for n in ("xin", "xout", "fin", "fout", "yd"))
        self.csem = K.dsem("cc")
        self.uid = 0
        with nc.allow_non_contiguous_dma(reason="small strided parameter loads"):
            self.build()
        self.es.close()

    def sb(self, st, shape, dtype, name=None):
        self.uid += 1
        t = st.enter_context(self.nc.sbuf_tensor(f"{name or 't'}_{self.uid}", shape, dtype))
        nb = int(np.prod(shape[1:])) * (4 if dtype == F32 else 2)
        nb = (nb + 31) // 32 * 32
        self.cur = getattr(self, "cur", 0) + nb
        self.peak = max(getattr(self, "peak", 0), self.cur)
        assert self.cur <= SBUF_BUDGET, f"SBUF budget exceeded at {name}: {self.cur}"

        def _dec(nb=nb):
            self.cur -= nb
        st.callback(_dec)
        return t

    def scr(self, st, name, shape, dtype):
        if not hasattr(st, "_scr"):
            st._scr = {}
        if name not in st._scr:
            st._scr[name] = (self.sb(st, shape, dtype, name), Buf(name))
        return st._scr[name]

    def collective(self, inp, out, b_in, b_out):
        K = self.K
        q = K.pool
        K._deps(q, [b_in], [b_out])
        self.csem.bind("cc")
        inst = self.nc.gpsimd.collective_compute(
            "AllGather", ALU.bypass, replica_groups=[[0, 1], [2, 3], [4, 5], [6, 7]],
            ins=[inp], outs=[out])
        self.csem.cnt += 1
        inst.then_inc(self.csem.sem, 1)
        K._record((self.csem, self.csem.cnt), [b_in], [b_out])

    def build(self):
        nc, K, es = self.nc, self.K, self.es
        st = es
        self.pb = [es.enter_context(nc.psum_tensor(f"pb{i}", [128, 512], F32)) for i in range(6)]
        self.pbb = [es.enter_context(nc.psum_tensor(f"pbb{i}", [128, 1024], BF16)) for i in range(2)]
        self.b_pb = [Buf(f"pb{i}", excl=True) for i in range(6)]
        self.b_pbb = [Buf(f"pbb{i}", excl=True) for i in range(2)]
        self.h = self.sb(st, [128, NT + NTC, D], F32, "h")
        self.b_h = [Buf(f"h{i}") for i in range(NT + NTC)]
        self.cn = {}
        self.b_cn = {}
        dsc = K.dsem("consts")
        for n in ("ident_bf", "ident_f", "rotT", "t1", "ccs", "halo_mask", "ones_f"):
            s, d = CONST_SPECS[n]
            self.cn[n] = self.sb(st, s, d, n)
            self.b_cn[n] = Buf(n)
            K.dma(K.sync, dsc, self.cn[n][:], self.Cd[n].ap(), [], [self.b_cn[n]])
        self.scl = self.sb(st, [128, 8, 64], F32, "scl")
        self.b_scl = Buf("scl")
        ctmp = self.sb(st, [128, 2, 8], F32, "ctmp")
        b_ctmp = Buf()
        K.dma(K.sync, dsc, ctmp[:], self.cvec.ap().rearrange("r (k p) -> p r k", p=128), [], [b_ctmp])
        K.op(K.pool, lambda: nc.gpsimd.memset(self.scl[:], 0.0), [], [self.b_scl])
        for r in range(2):
            K.op(K.act, lambda r=r: nc.scalar.activation(out=self.scl[:, :, 32 * r], in_=ctmp[:, r, :], func=AF.Silu),
                 [b_ctmp], [self.b_scl])
        dsx = K.dsem("xload")
        first = self.layers[0]
        for t in range(NT):
            K.dma(K.sync, dsx, self.h[:, t, :], self.x_in.ap()[t * 128:(t + 1) * 128, :], [], [self.b_h[t]])
        for t in range(NTC):
            K.dma(K.sync, dsx, self.h[:, NT + t, :], self.hc_in.ap()[t * 128:(t + 1) * 128, :], [], [self.b_h[NT + t]])
        STOPPED[0] = False
        if not ckpt("load"):
            for i in self.layers:
                if not STOPPED[0]:
                    self.layer(i)
                    K.barrier()
                    K.recycle()
        K.barrier()
        dso = K.dsem("store")
        for t in range(NT):
            K.dma(K.sync, dso, self.h_out.ap()[t * 128:(t + 1) * 128, :], self.h[:, t, :], [self.b_h[t]], [])
        for t in range(NTC):
            K.dma(K.sync, dso, self.hc_out.ap()[t * 128:(t + 1) * 128, :], self.h[:, NT + t, :], [self.b_h[NT + t]], [])
        K.sync.h.wait_ge(dso.sem, dso.cnt)
        K.barrier()

    def alloc_mods(self, st, st_m, use_ctx, bc_all):
        out = {}
        sufs = ["", "c"] if use_ctx else [""]
        for suf in sufs:
            out["g" + suf] = (self.sb(st, [128, D], F32, "bc_g" + suf), Buf())
        for suf in sufs:
            for nm in ("sh", "scp"):
                if bc_all:
                    out[nm + suf] = (self.sb(st_m, [128, D], F32, "bc_" + nm + suf), Buf())
                else:
                    out[nm + suf] = (self.sb(st_m, [128, 8], F32, "col_" + nm + suf), Buf())
        out["_bc_all"] = bc_all
        out["_sufs"] = sufs
        return out

    def mods(self, i, sub, out):
        nc, K = self.nc, self.K
        bc_all = out["_bc_all"]
        with ExitStack() as ls:
            modrow = self.sb(ls, [64, 3072], F32, "modrow")
            b_modrow = Buf()
            brow = self.sb(ls, [64, 3072], F32, "brow")
            b_brow = Buf()
            slots = [self.sb(ls, [128, 8, 256], F32, "adas") for _ in range(2)]
            b_slots = [Buf(), Buf()]
            ds = [K.dsem(f"ada{i}{sub}a"), K.dsem(f"ada{i}{sub}b")]
            dsb = K.dsem(f"adab{i}{sub}")
            c0 = (sub - 1) * 3072
            for r in (0, 32):
                K.dma(K.sync, dsb, brow[r:r + 1, :], self.W[("ada_b", i)].ap().rearrange("(o c) -> o c", o=1)[0:1, c0:c0 + 3072], [], [b_brow])

            def load_chunk(n):
                s = n % 2
                src = self.W[("ada_w", i)].ap()[:, c0 + n * 256:c0 + (n + 1) * 256].rearrange("(k p) c -> p k c", p=128)
                K.dma(K.sync, ds[s], slots[s][:], src, [], [b_slots[s]])

            load_chunk(0)
            for n in range(12):
                s = n % 2
                if n + 1 < 12:
                    load_chunk(n + 1)
                pbk = n % 2
                for k in range(8):
                    K.op(K.pe, lambda k=k, pbk=pbk, s=s: nc.tensor.matmul(
                        self.pb[pbk][0:33, 0:256], lhsT=self.scl[:, k, 0:33], rhs=slots[s][:, k, :],
                        start=(k == 0), stop=(k == 7)),
                        [self.b_scl, b_slots[s]], [self.b_pb[pbk]], mark=(k == 7))
                for r in (0, 32):
                    K.op(K.dve, lambda n=n, pbk=pbk, r=r: nc.vector.tensor_tensor(
                        out=modrow[r:r + 1, n * 256:(n + 1) * 256], in0=self.pb[pbk][r:r + 1, 0:256], in1=brow[r:r + 1, n * 256:(n + 1) * 256],
                        op=ALU.add), [self.b_pb[pbk], b_brow], [b_modrow])
            for r in (0, 32):
                K.op(K.dve, lambda r=r: nc.vector.tensor_scalar(out=modrow[r:r + 1, 1024:2048], in0=modrow[r:r + 1, 1024:2048],
                                                                 scalar1=1.0, scalar2=None, op0=ALU.add), [b_modrow], [b_modrow])
            names = ["sh", "scp", "g"]
            for suf in out["_sufs"]:
                r = 32 if suf == "c" else 0
                for v in range(3):
                    t, b = out[names[v] + suf]
                    if v == 2 or bc_all:
                        for hf in range(2):
                            pbk = 2 + hf
                            K.op(K.pe, lambda r=r, v=v, hf=hf, pbk=pbk: nc.tensor.matmul(
                                self.pb[pbk][:, :], lhsT=self.cn["ones_f"][r:r + 1, :],
                                rhs=modrow[r:r + 1, v * 1024 + hf * 512:v * 1024 + (hf + 1) * 512], start=True, stop=True),
                                [self.b_cn["ones_f"], b_modrow], [self.b_pb[pbk]], mark=True)
                            K.op(K.act, lambda t=t, hf=hf, pbk=pbk: nc.scalar.copy(out=t[:, hf * 512:(hf + 1) * 512], in_=self.pb[pbk][:, :]),
                                 [self.b_pb[pbk]], [b])
                    else:
                        pbk = 4 + (v % 2)
                        for k in range(8):
                            K.op(K.pe, lambda r=r, v=v, k=k, pbk=pbk: nc.tensor.matmul(
                                self.pb[pbk][:, k:k + 1], lhsT=modrow[r:r + 1, v * 1024 + k * 128:v * 1024 + (k + 1) * 128],
                                rhs=self.cn["ones_f"][r:r + 1, 0:1], start=True, stop=True, skip_group_check=True),
                                [self.b_cn["ones_f"], b_modrow], [self.b_pb[pbk]], mark=(k == 7))
                        K.op(K.act, lambda t=t, pbk=pbk: nc.scalar.copy(out=t[:, :], in_=self.pb[pbk][:, 0:8]), [self.b_pb[pbk]], [b])
            K.barrier()

    def ln_bc(self, st, i, sub):
        nc, K = self.nc, self.K
        ds = K.dsem(f"lnbc{i}{sub}")
        out = {}
        for nm in ("ln_g", "ln_b"):
            t = self.sb(st, [128, D], F32, nm)
            b = Buf()
            row = self.W[(nm, i)].ap()[sub - 1:sub, :]
            src = bass.AP(row.tensor, row.offset, [[0, 128], [1, D]])
            K.dma(K.sync, ds, t[:], src, [], [b])
            out[nm] = (t, b)
        return out

    def ln_stats(self, st, src, b_src, slot=0):
        nc, K = self.nc, self.K
        stats, b = self.scr(st, f"stats{slot}", [128, 2, 6], F32)
        mv, _ = self.scr(st, f"mv{slot}", [128, 2], F32)
        rstd, _ = self.scr(st, f"rstd{slot}", [128, 2], F32)
        for hf in range(2):
            K.op(K.dve, lambda hf=hf: nc.vector.bn_stats(out=stats[:, hf, :], in_=src[:, hf * 512:(hf + 1) * 512]), [b_src], [b])
        K.op(K.dve, lambda: nc.vector.bn_aggr(out=mv[:], in_=stats[:].rearrange("p a b -> p (a b)")), [b], [b])
        K.op(K.pool, lambda: nc.gpsimd.tensor_scalar(out=rstd[:, 0:1], in0=mv[:, 1:2], scalar1=1.0, scalar2=LN_EPS, op0=ALU.mult, op1=ALU.add), [b], [b])
        K.op(K.pool, lambda: nc.gpsimd.tensor_tensor(out=rstd[:, 0:1], in0=rstd[:, 0:1], in1=self.eps_t[:, 2:3], op=ALU.pow), [b, self.b_eps], [b])
        K.op(K.dve, lambda: nc.vector.tensor_scalar(out=rstd[:, 1:2], in0=mv[:, 0:1], scalar1=rstd[:, 0:1], scalar2=-1.0,
                                                     op0=ALU.mult, op1=ALU.mult), [b], [b])
        return mv, rstd, b

    def modulate_tok(self, ls, t, bc, suf, out_ap, b_out):
        nc, K = self.nc, self.K
        src = self.h[:, t, :]
        mv, rstd, b = self.ln_stats(ls, src, self.b_h[t])
        tmp, bt = self.scr(ls, "modtmp", [128, D], F32)
        scp, b_scp = bc["scp" + suf]
        sh, b_sh = bc["sh" + suf]
        K.op(K.dve, lambda: nc.vector.scalar_tensor_tensor(out=tmp[:], in0=src, scalar=mv[:, 0:1], in1=scp[:],
                                                            op0=ALU.subtract, op1=ALU.mult), [self.b_h[t], b, b_scp], [bt])
        K.op(K.dve, lambda: nc.vector.scalar_tensor_tensor(out=out_ap, in0=tmp[:], scalar=rstd[:, 0:1], in1=sh[:],
                                                            op0=ALU.mult, op1=ALU.add), [bt, b, b_sh], [b_out])

    def modulate_T(self, ls, t, bc, suf, xb, b_xb, dst, b_dst, col0, pbk):
        nc, K = self.nc, self.K
        src = self.h[:, t, :]
        mv, rstd, b = self.ln_stats(ls, src, self.b_h[t], slot=pbk)
        K.op(K.act, lambda: nc.scalar.activation(out=xb[:], in_=src, func=AF.Identity, bias=rstd[:, 1:2], scale=rstd[:, 0:1]),
             [self.b_h[t], b], [b_xb])
        scp, b_scp = bc["scp" + suf]
        sh, b_sh = bc["sh" + suf]
        for k in range(8):
            K.op(K.pe, lambda k=k: nc.tensor.transpose(out=self.pbb[pbk][:, k * 128:(k + 1) * 128], in_=xb[:, k * 128:(k + 1) * 128],
                                                        identity=self.cn["ident_bf"][:]),
                 [b_xb, self.b_cn["ident_bf"]], [self.b_pbb[pbk]], mark=(k == 7))
        for k in range(8):
            K.op(K.act, lambda k=k: nc.scalar.activation(out=dst[:, k, col0:col0 + 128], in_=self.pbb[pbk][:, k * 128:(k + 1) * 128],
                                                          func=AF.Identity, bias=sh[:, k:k + 1], scale=scp[:, k:k + 1]),
                 [self.b_pbb[pbk], b_scp, b_sh], [b_dst])

    def residual_ln(self, ls, t, y_aps, b_ys, bc, suf, lnb, first=True, last=True):
        nc, K = self.nc, self.K
        g, b_g = bc["g" + suf]
        hh = self.h[:, t, :]
        tmp, bt = self.scr(ls, "restmp", [128, D], F32)
        for hf in range(2):
            K.op(K.dve, lambda hf=hf: nc.vector.tensor_tensor(out=tmp[:, hf * 512:(hf + 1) * 512], in0=y_aps[hf],
                                                               in1=g[:, hf * 512:(hf + 1) * 512], op=ALU.mult),
                 [b_ys[hf], b_g], [bt])
        K.op(K.dve, lambda: nc.vector.scalar_tensor_tensor(out=hh, in0=hh, scalar=(ALPHA if first else 1.0), in1=tmp[:],
                                                            op0=ALU.mult, op1=ALU.add), [bt, self.b_h[t]], [self.b_h[t]])
        if not last:
            return
        mv, rstd, b = self.ln_stats(ls, hh, self.b_h[t])
        lg, b_lg = lnb["ln_g"]
        lb, b_lb = lnb["ln_b"]
        K.op(K.dve, lambda: nc.vector.scalar_tensor_tensor(out=tmp[:], in0=hh, scalar=mv[:, 0:1], in1=lg[:],
                                                            op0=ALU.subtract, op1=ALU.mult), [self.b_h[t], b, b_lg], [bt])
        K.op(K.dve, lambda: nc.vector.scalar_tensor_tensor(out=hh, in0=tmp[:], scalar=rstd[:, 0:1], in1=lb[:],
                                                            op0=ALU.mult, op1=ALU.add), [bt, b, b_lb], [self.b_h[t]])

    def layer(self, i):
        nc, K = self.nc, self.K
        j = i // 2
        ctx_upd = i < 2
        ctx_in = i <= 2
        ntl = NT + (NTC if ctx_in else 0)
        ntu = NT + (NTC if ctx_upd else 0)
        if not hasattr(self, "eps_t"):
            self.eps_t = self.sb(self.es, [128, 4], F32, "eps")
            self.b_eps = Buf()
            K.op(K.pool, lambda: nc.gpsimd.memset(self.eps_t[:, 0:1], LN_EPS), [], [self.b_eps])
            K.op(K.pool, lambda: nc.gpsimd.memset(self.eps_t[:, 1:2], SUBLN_EPS), [], [self.b_eps])
            K.op(K.pool, lambda: nc.gpsimd.memset(self.eps_t[:, 2:3], -0.5), [], [self.b_eps])
            K.barrier()
        with ExitStack() as lst:
            if i % 2 == 0:
                bc = self.alloc_mods(lst, lst, ctx_in, bc_all=False)
                self.mods(i, 1, bc)
                if ckpt("mods"):
                    return
                self.even_mixer(lst, i, j, bc, ctx_upd, ntl, ntu)
            else:
                self.odd_mixer(lst, i, j, ctx_upd, ntl, ntu)
            K.barrier()
        if STOPPED[0] or ckpt("mixer"):
            return
        with ExitStack() as lst:
            bc = self.alloc_mods(lst, lst, ctx_upd, bc_all=False)
            self.mods(i, 2, bc)
            lnb = self.ln_bc(lst, i, 2)
            self.mlp(lst, i, bc, lnb, ntu)
            K.barrier()

    def build_uT(self, ntl, bc, uT, b_uT):
        K = self.K
        with ExitStack() as ls:
            xb = [self.sb(ls, [128, D], BF16, "xb") for _ in range(2)]
            b_xb = [Buf(), Buf()]
            for t in range(ntl):
                if True:
                    l2 = ls
                    s = t % 2
                    self.modulate_T(l2, t, bc, "c" if t >= NT else "", xb[s], b_xb[s], uT, b_uT[t], t * 128, s)
            K.barrier()

    def mlp(self, st, i, bc, lnb, ntu):
        nc, K = self.nc, self.K
        ncols = ntu * 128
        uT = self.sb(st, [128, 8, ncols], BF16, "uT")
        b_uT = [Buf() for _ in range(ntu)]
        self.build_uT(ntu, bc, uT, b_uT)
        ring = [self.sb(st, [128, 8, 1024], BF16, "wring") for _ in range(3)]
        b_ring = [Buf() for _ in range(3)]
        dsw = [K.dsem(f"mlpw{i}{k}") for k in range(3)]
        h1 = self.sb(st, [128, 8, 512], BF16, "h1")
        b_h1 = Buf()
        groups = []
        t0 = 0
        while t0 < ntu:
            n = min(4, ntu - t0)
            groups.append((t0, n))
            t0 += n

        def load(k):
            q, which = k // 2, k % 2
            if q >= 4:
                return
            s = k % 3
            if which == 0:
                src_ = self.W[("mlp_w1", i)].ap()[:, q * 1024:(q + 1) * 1024].rearrange("(k p) c -> p k c", p=128)
            else:
                src_ = self.W[("mlp_w2", i)].ap()[q * 1024:(q + 1) * 1024, :].rearrange("(k p) c -> p k c", p=128)
            K.dma(K.pool, dsw[s], ring[s][:], src_, [], [b_ring[s]])

        load(0)
        load(1)
        load(2)
        for q in range(4):
            w1, bw1 = ring[(2 * q) % 3], b_ring[(2 * q) % 3]
            w2, bw2 = ring[(2 * q + 1) % 3], b_ring[(2 * q + 1) % 3]
            for gidx, (t0, n) in enumerate(groups):
                w = n * 128
                for f in range(8):
                    pbk = f % 2
                    for k in range(8):
                        K.op(K.pe, lambda f=f, k=k, pbk=pbk: nc.tensor.matmul(
                            self.pb[pbk][:, 0:w], lhsT=w1[:, k, f * 128:(f + 1) * 128], rhs=uT[:, k, t0 * 128:t0 * 128 + w],
                            start=(k == 0), stop=(k == 7)),
                            [bw1] + b_uT[t0:t0 + n], [self.b_pb[pbk]], mark=(k == 7))
                    K.op(K.act, lambda f=f, pbk=pbk: nc.scalar.activation(out=h1[:, f, 0:w], in_=self.pb[pbk][:, 0:w], func=AF.Relu),
                         [self.b_pb[pbk]], [b_h1])
                    K.op(K.act, lambda f=f: nc.scalar.activation(out=h1[:, f, 0:w], in_=h1[:, f, 0:w], func=AF.Square), [b_h1], [b_h1])
                if gidx == len(groups) - 1:
                    load(2 * q + 3)
                for tt in range(n):
                    t = t0 + tt
                    pk = [2 + 2 * (t % 2), 3 + 2 * (t % 2)]
                    for hf in range(2):
                        for f in range(8):
                            K.op(K.pe, lambda f=f, hf=hf, tt=tt: nc.tensor.matmul(
                                self.pb[pk[hf]][:, :], lhsT=h1[:, f, tt * 128:(tt + 1) * 128], rhs=w2[:, f, hf * 512:(hf + 1) * 512],
                                start=(f == 0), stop=(f == 7)),
                                [b_h1, bw2], [self.b_pb[pk[hf]]], mark=(f == 7))
                    self.residual_ln(st, t, [self.pb[pk[0]][:, :], self.pb[pk[1]][:, :]], [self.b_pb[pk[0]], self.b_pb[pk[1]]],
                                     bc, "c" if t >= NT else "", lnb, first=(q == 0), last=(q == 3))
            load(2 * q + 4)

    def even_mixer(self, st, i, j, bc, ctx_upd, ntl, ntu):
        nc, K = self.nc, self.K
        ncols = ntl * 128
        lam_init = 0.8 - 0.6 * math.exp(-0.3 * i)
        QT = self.sb(st, [128, 4, TOKA], BF16, "QT")
        b_QT = Buf()
        slT = self.sb(st, [128, 4, TOKA], BF16, "slT")
        b_sl = Buf()
        kc = self.sb(st, [128, 4, 256], BF16, "kc")
        vc = self.sb(st, [128, 2, 512], BF16, "vc")
        b_kc, b_vc = Buf(), Buf()
        small = self.sb(st, [128, 64], F32, "small")
        b_small = Buf()
        cw = self.sb(st, [128, 4, 3], F32, "cw")
        b_cw = Buf()
        gbh = self.sb(st, [128, 4, 2], F32, "gbh")
        b_gbh = Buf()
        dsm = K.dsem(f"evsm{i}")
        for tap in range(3):
            K.dma(K.sync, dsm, cw[:, :, tap], self.W[("conv_w", i)].ap()[tap, :].rearrange("(c p) -> p c", p=128), [], [b_cw])
        lq = self.sb(st, [128, 4, 64], F32, "lq")
        b_lq = Buf()
        lsrc = self.W[("lambda_qk", i)].ap()
        K.dma(K.sync, dsm, lq[:], bass.AP(lsrc.tensor, lsrc.offset, [[0, 128], [64, 4], [1, 64]]), [], [b_lq])
        sg = self.sb(st, [128, 128], F32, "sg")
        b_sg = Buf()
        ssrc = self.W[("subln_g", i)].ap()
        K.dma(K.sync, dsm, sg[:], bass.AP(ssrc.tensor, ssrc.offset, [[0, 128], [1, 128]]), [], [b_sg])
        ltmp = self.sb(st, [128, 2, 64], F32, "ltmp")
        b_lt = Buf()
        for a in range(2):
            K.op(K.dve, lambda a=a: nc.vector.tensor_tensor(out=ltmp[:, a, :], in0=lq[:, 2 * a, :], in1=lq[:, 2 * a + 1, :], op=ALU.mult),
                 [b_lq], [b_lt])
            K.op(K.dve, lambda a=a: nc.vector.tensor_reduce(out=small[:, a:a + 1], in_=ltmp[:, a, :], op=ALU.add,
                                                             axis=mybir.AxisListType.X), [b_lt], [b_small])
        K.op(K.act, lambda: nc.scalar.activation(out=small[:, 2:4], in_=small[:, 0:2], func=AF.Exp), [b_small], [b_small])
        K.op(K.dve, lambda: nc.vector.scalar_tensor_tensor(out=small[:, 4:5], in0=small[:, 3:4], scalar=-lam_init, in1=small[:, 2:3],
                                                            op0=ALU.add, op1=ALU.subtract), [b_small], [b_small])
        K.op(K.dve, lambda: nc.vector.tensor_scalar(out=sg[:], in0=sg[:], scalar1=(1.0 - lam_init), scalar2=None, op0=ALU.mult),
             [b_sg], [b_sg])

        groups = [(g * 4, 4) for g in range(4)] + ([(16, 2)] if ntl > NT else [])
        with ExitStack() as sa:
            uT = self.sb(sa, [128, 8, ncols], BF16, "uT")
            b_uT = [Buf() for _ in range(ntl)]
            self.build_uT(ntl, bc, uT, b_uT)
            if ckpt("uT"):
                return
            with ExitStack() as ls:
                gv = self.sb(ls, [128, 4, TOKA + 4], BF16, "gv")
                b_gv = Buf()
                K.op(K.pool, lambda: nc.gpsimd.memset(gv[:], 0.0), [], [b_gv])
                cnt = 0
                for hh in range(2):
                    with ExitStack() as l3:
                        wc = self.sb(l3, [128, 8, 2, 256], BF16, "wconv")
                        b_wc = Buf()
                        dsw = K.dsem(f"wconv{i}{hh}")
                        for blk in range(2):
                            c_lo = 512 + blk * 512 + hh * 256
                            K.dma(K.pool, dsw, wc[:, :, blk, :], self.W[("w_in", i)].ap()[:, c_lo:c_lo + 256].rearrange("(k p) c -> p k c", p=128),
                                  [], [b_wc])
                        gcT = [self.sb(l3, [128, 512], BF16, "gcT") for _ in range(2)]
                        b_gc = [Buf(), Buf()]
                        for (t0, n) in groups:
                            w = n * 128
                            c0 = t0 * 128 + 1 + (2 if t0 >= NT else 0)
                            for c in (2 * hh, 2 * hh + 1):
                                s = cnt % 2
                                cnt += 1
                                for blk, pbk in ((0, 0), (1, 1)):
                                    for k in range(8):
                                        K.op(K.pe, lambda k=k, blk=blk, pbk=pbk, c=c: nc.tensor.matmul(
                                            self.pb[pbk][:, 0:w], lhsT=wc[:, k, blk, (c % 2) * 128:(c % 2 + 1) * 128],
                                            rhs=uT[:, k, t0 * 128:t0 * 128 + w], start=(k == 0), stop=(k == 7)),
                                            [b_wc] + b_uT[t0:t0 + n], [self.b_pb[pbk]], mark=(k == 7))
                                K.op(K.act, lambda s=s: nc.scalar.copy(out=gcT[s][:, 0:w], in_=self.pb[0][:, 0:w]), [self.b_pb[0]], [b_gc[s]])
                                K.op(K.dve, lambda s=s, c=c, c0=c0: nc.vector.tensor_tensor(out=gv[:, c, c0:c0 + w], in0=self.pb[1][:, 0:w], in1=gcT[s][:, 0:w],
                                                                                             op=ALU.mult), [self.b_pb[1], b_gc[s]], [b_gv])
                        K.barrier()
                dsh = K.dsem(f"halo{i}")
                hrow = self.xh_in.ap()[0:1, 0:1024]
                for wi, colv in ((0, 1), (1, TOK), (2, 1), (3, TOK)):
                    dst = bass.AP(hrow.tensor, hrow.offset + wi * 512, [[1, 128], [128, 4], [1, 1]])
                    K.dma(K.sync, dsh, dst, gv[:, :, colv:colv + 1], [b_gv], [self.b_xin])
                with ExitStack() as l3:
                    wc = self.sb(l3, [128, 8, 512], BF16, "wgb")
                    b_wc = Buf()
                    dsw = K.dsem(f"wgb{i}")
                    K.dma(K.pool, dsw, wc[:], self.W[("w_in", i)].ap()[:, 0:512].rearrange("(k p) c -> p k c", p=128), [], [b_wc])
                    ctmp = [self.sb(l3, [128, 512], F32, "ctmp")] * 2
                    b_ct = [Buf()] * 2
                    cms = [self.sb(l3, [128, 512], BF16, "cms")] * 2
                    b_cms = [Buf()] * 2
                    dscm = K.dsem(f"cmd{i}")
                    for (t0, n) in groups:
                        w = n * 128
                        c0 = t0 * 128 + 1 + (2 if t0 >= NT else 0)
                        for c in range(4):
                            s = cnt % 2
                            cnt += 1
                            pbk = 2 + s
                            for k in range(8):
                                K.op(K.pe, lambda k=k, pbk=pbk, c=c: nc.tensor.matmul(
                                    self.pb[pbk][:, 0:w], lhsT=wc[:, k, c * 128:(c + 1) * 128],
                                    rhs=uT[:, k, t0 * 128:t0 * 128 + w], start=(k == 0), stop=(k == 7)),
                                    [b_wc] + b_uT[t0:t0 + n], [self.b_pb[pbk]], mark=(k == 7))
                            K.op(K.dve, lambda s=s, c=c, c0=c0: nc.vector.tensor_scalar(out=ctmp[s][:, 0:w], in0=gv[:, c, c0:c0 + w], scalar1=cw[:, c, 1:2],
                                                                                         scalar2=None, op0=ALU.mult), [b_gv, b_cw], [b_ct[s]])
                            K.op(K.dve, lambda s=s, c=c, c0=c0: nc.vector.scalar_tensor_tensor(out=ctmp[s][:, 0:w], in0=gv[:, c, c0 - 1:c0 - 1 + w], scalar=cw[:, c, 0:1],
                                                                                                in1=ctmp[s][:, 0:w], op0=ALU.mult, op1=ALU.add), [b_gv, b_cw, b_ct[s]], [b_ct[s]])
                            K.op(K.dve, lambda s=s, c=c, c0=c0: nc.vector.scalar_tensor_tensor(out=ctmp[s][:, 0:w], in0=gv[:, c, c0 + 1:c0 + 1 + w], scalar=cw[:, c, 2:3],
                                                                                                in1=ctmp[s][:, 0:w], op0=ALU.mult, op1=ALU.add), [b_gv, b_cw, b_ct[s]], [b_ct[s]])
                            K.op(K.dve, lambda s=s, c=c, pbk=pbk: nc.vector.tensor_tensor(out=cms[s][:, 0:w], in0=self.pb[pbk][:, 0:w], in1=ctmp[s][:, 0:w],
                                                                                           op=ALU.mult), [self.b_pb[pbk], b_ct[s]], [b_cms[s]])
                            K.dma(K.sync, dscm, self.cmd.ap()[:, c * TOKA + t0 * 128:c * TOKA + t0 * 128 + w], cms[s][:, 0:w], [b_cms[s]], [self.b_cmd])
                            if t0 == 0:
                                K.op(K.act, lambda c=c, pbk=pbk: nc.scalar.copy(out=gbh[:, c, 0:1], in_=self.pb[pbk][:, 0:1]), [self.b_pb[pbk]], [b_gbh])
                            if t0 == 12:
                                K.op(K.act, lambda c=c, pbk=pbk: nc.scalar.copy(out=gbh[:, c, 1:2], in_=self.pb[pbk][:, 511:512]), [self.b_pb[pbk]], [b_gbh])
                    K.barrier()
            if ckpt("A2"):
                return
            dsx = K.dsem(f"xst{i}")
            with ExitStack() as ls:
                cs = [self.sb(ls, [128, TOK], BF16, n) for n in ("cos", "sin")]
                b_cs = Buf()
                dscs = K.dsem(f"cs{i}")
                for a, n in enumerate(("cos", "sin")):
                    K.dma(K.sync, dscs, cs[a][:], self.Cd[n].ap(), [], [b_cs])
                stg = self.sb(ls, [128, 4, 512], BF16, "stg")
                b_stg = Buf()
                qb = [self.sb(ls, [128, 512], BF16, "qb") for _ in range(2)]
                b_qb = [Buf(), Buf()]
                r1 = self.sb(ls, [128, 512], F32, "r1")
                b_r = Buf()
                cnt = 0
                for blk in range(2):
                    with ExitStack() as l3:
                        wq = self.sb(l3, [128, 8, 512], BF16, "wqk")
                        b_wq = Buf()
                        dsw = K.dsem(f"wqk{i}{blk}")
                        K.dma(K.pool, dsw, wq[:], self.W[("w_in", i)].ap()[:, 1536 + blk * 512:2048 + blk * 512].rearrange("(k p) c -> p k c", p=128),
                              [], [b_wq])
                        for gidx, (t0, n) in enumerate(groups):
                            w = n * 128
                            isctx = t0 >= NT
                            if isctx and blk == 0 and not ctx_upd:
                                continue
                            for hd in range(4):
                                s = cnt % 2
                                cnt += 1
                                pbk = s
                                for k in range(8):
                                    K.op(K.pe, lambda k=k, pbk=pbk, hd=hd: nc.tensor.matmul(
                                        self.pb[pbk][:, 0:w], lhsT=wq[:, k, hd * 128:(hd + 1) * 128],
                                        rhs=uT[:, k, t0 * 128:t0 * 128 + w], start=(k == 0), stop=(k == 7)),
                                        [b_wq] + b_uT[t0:t0 + n], [self.b_pb[pbk]], mark=(k == 7))
                                if isctx:
                                    dst = QT[:, hd, TOK:TOK + w] if blk == 0 else kc[:, hd, :]
                                    bd = b_QT if blk == 0 else b_kc
                                    K.op(K.act, lambda dst=dst, pbk=pbk: nc.scalar.copy(out=dst, in_=self.pb[pbk][:, 0:w]), [self.b_pb[pbk]], [bd])
                                    continue
                                if "v1" in DBG_SKIP:
                                    K.op(K.act, lambda s=s, pbk=pbk: nc.scalar.copy(out=qb[s][:, :], in_=self.pb[pbk][:, :]), [self.b_pb[pbk]], [b_qb[s]])
                                    continue
                                if "v6" in DBG_SKIP:
                                    K.op(K.act, lambda s=s, pbk=pbk: nc.scalar.activation(out=qb[s][:, :], in_=self.pb[pbk][:, :], func=AF.Identity), [self.b_pb[pbk]], [b_qb[s]])
                                    continue
                                if "v7" in DBG_SKIP:
                                    K.op(K.act, lambda s=s, pbk=pbk, hd=hd: nc.scalar.copy(out=stg[:, hd, :], in_=self.pb[pbk][:, :]), [self.b_pb[pbk]], [b_stg])
                                    continue
                                if "v8" in DBG_SKIP:
                                    K.op(K.act, lambda s=s, pbk=pbk: nc.scalar.copy(out=qb[0][:, :], in_=self.pb[pbk][:, :]), [self.b_pb[pbk]], [b_qb[0]])
                                    continue
                                if "v2" in DBG_SKIP:
                                    K.op(K.act, lambda s=s, pbk=pbk: nc.scalar.copy(out=qb[s][:, :], in_=self.pb[pbk][:, :]), [self.b_pb[pbk]], [b_qb[s]])
                                    K.op(K.pe, lambda s=s: nc.tensor.matmul(self.pb[2 + s][:, :], lhsT=self.cn["rotT"][:], rhs=qb[s][:, :], start=True, stop=True),
                                         [self.b_cn["rotT"], b_qb[s]], [self.b_pb[2 + s]], mark=True)
                                    continue
                                if "v3" in DBG_SKIP:
                                    K.op(K.dve, lambda s=s, pbk=pbk: nc.vector.tensor_tensor(out=r1[:], in0=self.pb[pbk][:, :], in1=cs[0][:, t0 * 128:t0 * 128 + 512], op=ALU.mult),
                                         [self.b_pb[pbk], b_cs], [b_r])
                                    continue
                                if "v4" in DBG_SKIP:
                                    K.op(K.dve, lambda s=s, pbk=pbk: nc.vector.tensor_copy(out=r1[:], in_=self.pb[pbk][:, :]),
                                         [self.b_pb[pbk]], [b_r])
                                    continue
                                K.op(K.dve, lambda s=s, pbk=pbk: nc.vector.tensor_copy(out=qb[s][:, :], in_=self.pb[pbk][:, :]), [self.b_pb[pbk]], [b_qb[s]])
                                K.op(K.pe, lambda s=s: nc.tensor.matmul(self.pb[2 + s][:, :], lhsT=self.cn["rotT"][:], rhs=qb[s][:, :], start=True, stop=True),
                                     [self.b_cn["rotT"], b_qb[s]], [self.b_pb[2 + s]], mark=True)
                                K.op(K.dve, lambda s=s, pbk=pbk: nc.vector.tensor_tensor(out=r1[:], in0=self.pb[pbk][:, :], in1=cs[0][:, t0 * 128:t0 * 128 + 512], op=ALU.mult),
                                     [self.b_pb[pbk], b_cs], [b_r])
                                K.op(K.dve, lambda s=s: nc.vector.tensor_tensor(out=self.pb[2 + s][:, :], in0=self.pb[2 + s][:, :], in1=cs[1][:, t0 * 128:t0 * 128 + 512], op=ALU.mult),
                                     [self.b_pb[2 + s], b_cs], [self.b_pb[2 + s]])
                                if blk == 0:
                                    K.op(K.dve, lambda s=s, hd=hd: nc.vector.tensor_tensor(out=QT[:, hd, t0 * 128:t0 * 128 + 512], in0=self.pb[2 + s][:, :], in1=r1[:], op=ALU.add),
                                         [b_r, self.b_pb[2 + s]], [b_QT])
                                else:
                                    K.op(K.dve, lambda s=s, hd=hd: nc.vector.tensor_tensor(out=stg[:, hd, :], in0=self.pb[2 + s][:, :], in1=r1[:], op=ALU.add),
                                         [b_r, self.b_pb[2 + s]], [b_stg])
                            if blk == 1 and not isctx:
                                K.dma(K.sync, dsx, self.xk_in.ap()[0:512, t0 * 128:t0 * 128 + 512].rearrange("(h p) c -> p h c", p=128), stg[:],
                                      [b_stg], [self.b_xin])
                        K.barrier()
            if ckpt("A3a"):
                return
            with ExitStack() as ls:
                wv = self.sb(ls, [128, 8, 512], BF16, "wv")
                b_wv = Buf()
                dsw = K.dsem(f"wv{i}")
                K.dma(K.pool, dsw, wv[:], self.W[("w_in", i)].ap()[:, 2560:3072].rearrange("(k p) c -> p k c", p=128), [], [b_wv])
                stg = [self.sb(ls, [128, 4, 512], BF16, "stgv") for _ in range(2)]
                b_stg = [Buf(), Buf()]
                for gidx, (t0, n) in enumerate(groups):
                    isctx = t0 >= NT
                    ss = gidx % 2
                    for tt in range(n):
                        pbk = 4 + (tt % 2)
                        for k in range(8):
                            K.op(K.pe, lambda k=k, pbk=pbk, tt=tt: nc.tensor.matmul(
                                self.pb[pbk][:, :], lhsT=uT[:, k, (t0 + tt) * 128:(t0 + tt + 1) * 128], rhs=wv[:, k, :],
                                start=(k == 0), stop=(k == 7)), [b_wv, b_uT[t0 + tt]], [self.b_pb[pbk]], mark=(k == 7))
                        if isctx:
                            K.op(K.act, lambda tt=tt, pbk=pbk: nc.scalar.copy(out=vc[:, tt, :], in_=self.pb[pbk][:, :]), [self.b_pb[pbk]], [b_vc])
                        else:
                            K.op(K.act, lambda tt=tt, pbk=pbk, ss=ss: nc.scalar.copy(out=stg[ss][:, tt, :], in_=self.pb[pbk][:, :]), [self.b_pb[pbk]], [b_stg[ss]])
                    if not isctx:
                        vdst = self.xv_in.ap()[0:1, 0:1]
                        dst = bass.AP(vdst.tensor, vdst.offset + t0 * 128 * 512, [[512, 128], [128 * 512, 4], [1, 512]])
                        K.dma(K.sync, dsx, dst, stg[ss][:], [b_stg[ss]], [self.b_xin])
                K.barrier()
        if ckpt("A3"):
            return
        self.collective(self.xk_in.ap(), self.xk_out.ap(), self.b_xin, self.b_xout)
        self.collective(self.xv_in.ap(), self.xv_out.ap(), self.b_xin, self.b_xout)
        self.collective(self.xh_in.ap(), self.xh_out.ap(), self.b_xin, self.b_xout)
        if ckpt("xchg"):
            return
        with ExitStack() as ls:
            KT = self.sb(ls, [128, 2, NKT * 128], BF16, "KT")
            b_KT = Buf()
            VA = self.sb(ls, [128, NKT, 2, 130], BF16, "VA")
            b_VA = Buf()
            dsa = K.dsem(f"attl{i}")
            K.op(K.pool, lambda: nc.gpsimd.memset(VA[:, :, :, 128:130], 1.0), [], [b_VA])
            xko = self.xk_out.ap()
            xvo = self.xv_out.ap()
            PT = [self.sb(ls, [128, 512], BF16, "PT") for _ in range(3)]
            b_PT = [Buf() for _ in range(3)]
            osb = self.sb(ls, [128, 2, 130], F32, "osb")
            b_osb = Buf()
            o1 = self.sb(ls, [128, 128], F32, "o1")
            o2 = self.sb(ls, [128, 128], F32, "o2")
            ob = self.sb(ls, [128, 128], BF16, "ob")
            sq = self.sb(ls, [128, 128], F32, "sq")
            sc_ = self.sb(ls, [128, 8], F32, "sc_")
            b_o = Buf()
            qgroups = [(g * 512, 512, list(range(NKT))) for g in range(4)]
            if ctx_upd:
                qgroups.append((TOK, 256, [32, 33]))
            pcnt = 0
            Qm = [self.sb(ls, [128, 2, 512], BF16, "Qm") for _ in range(2)]
            b_Qm = [Buf(), Buf()]
            for k in range(2):
                K.op(K.pool, lambda k=k: nc.gpsimd.memset(Qm[k][:], 0.0), [], [b_Qm[k]])
            qcnt = 0
            Sb = [self.pb[0][:, :], self.pb[1][:, :], self.pbb[1][:].bitcast(F32)]
            b_Sb = [self.b_pb[0], self.b_pb[1], self.b_pbb[1]]
            for hp in range(2):
                for r in range(2):
                    K.dma(K.sync, dsa, KT[:, :, r * TOK:(r + 1) * TOK],
                          xko[r * 512 + hp * 256:r * 512 + hp * 256 + 256, :].rearrange("(h p) c -> p h c", p=128), [self.b_xout], [b_KT])
                    for h2 in range(2):
                        hd = hp * 2 + h2
                        base = xvo[r * 512:r * 512 + 1, 0:1]
                        src = bass.AP(base.tensor, base.offset + hd * 128, [[512, 128], [128 * 512, 16], [1, 128]])
                        K.dma(K.sync, dsa, VA[:, r * 16:(r + 1) * 16, h2, 0:128], src, [self.b_xout], [b_VA])
                K.op(K.act, lambda hp=hp: nc.scalar.copy(out=KT[:, :, 2 * TOK:2 * TOK + 256], in_=kc[:, hp * 2:hp * 2 + 2, :]), [b_kc], [b_KT])
                for tt in range(2):
                    K.op(K.act, lambda tt=tt, hp=hp: nc.scalar.copy(out=VA[:, 32 + tt, :, 0:128],
                                                                   in_=vc[:, tt, hp * 256:(hp + 1) * 256].rearrange("p (h d) -> p h d", h=2)),
                         [b_vc], [b_VA])
                heads = [(q0, qw, kts, h2) for (q0, qw, kts) in qgroups for h2 in range(2)]
                qbase = qcnt

                def emit_qm(k):
                    q0_, qw_, _, h2_ = heads[k]
                    qs_ = (qbase + k) % 2
                    for c in range(2):
                        K.op(K.pool, lambda c=c, qs_=qs_, q0_=q0_, qw_=qw_, h2_=h2_: nc.gpsimd.tensor_copy(
                            out=Qm[qs_][c * 64:(c + 1) * 64, c, 0:qw_], in_=QT[c * 64:(c + 1) * 64, hp * 2 + h2_, q0_:q0_ + qw_]),
                            [b_QT], [b_Qm[qs_]])

                emit_qm(0)
                qcnt += len(heads)
                for hk, (q0, qw, kts, h2) in enumerate(heads):
                    if True:
                        nqt = qw // 128
                        hd = hp * 2 + h2
                        qs = (qbase + hk) % 2
                        its = [(ki, kt, c) for ki, kt in enumerate(kts) for c in range(2)]

                        def emit_s(n):
                            ki, kt, c = its[n]
                            sp = (pbase + n) % 3
                            K.op(K.pe, lambda c=c, kt=kt, sp=sp: nc.tensor.matmul(
                                Sb[sp][:, 0:qw], lhsT=KT[:, h2, kt * 128:(kt + 1) * 128],
                                rhs=Qm[qs][:, c, 0:qw], start=True, stop=True),
                                [b_KT, b_Qm[qs]], [b_Sb[sp]], mark=True)

                        pbase = pcnt
                        emit_s(0)
                        if len(its) > 1:
                            emit_s(1)
                        for n, (ki, kt, c) in enumerate(its):
                            sp = (pbase + n) % 3
                            pp = (pbase + n) % 3
                            if n + 2 < len(its):
                                emit_s(n + 2)
                            K.op(K.act, lambda sp=sp, pp=pp: nc.scalar.activation(out=PT[pp][:, 0:qw], in_=Sb[sp][:, 0:qw], func=AF.Exp, scale=0.125),
                                 [b_Sb[sp]], [b_PT[pp]])
                            for qt in range(nqt):
                                K.op(K.pe, lambda c=c, kt=kt, qt=qt, pp=pp, ki=ki: nc.tensor.matmul(
                                    self.pb[2 + qt][:, c * 256:c * 256 + 129], lhsT=PT[pp][:, qt * 128:(qt + 1) * 128],
                                    rhs=VA[:, kt, h2, 0:129], start=(ki == 0 and c == 0), stop=(ki == len(kts) - 1),
                                    skip_group_check=True),
                                    [b_PT[pp], b_VA], [self.b_pb[2 + qt]], mark=(ki == len(kts) - 1 and c == 1))
                        pcnt += len(its)
                        if hk + 1 < len(heads):
                            emit_qm(hk + 1)
                        for qt in range(nqt):
                            bk = 2 + qt
                            K.op(K.act, lambda bk=bk: nc.scalar.copy(out=osb[:, :, 0:129],
                                                                      in_=self.pb[bk][:].rearrange("p (c x) -> p c x", c=2)[:, :, 0:129]),
                                 [self.b_pb[bk]], [b_osb])
                            K.op(K.dve, lambda: nc.vector.reciprocal(out=sc_[:, 0:2], in_=osb[:, :, 128]), [b_osb], [b_o])
                            K.op(K.dve, lambda: nc.vector.tensor_tensor(out=sc_[:, 2:3], in0=sc_[:, 1:2], in1=small[:, 4:5], op=ALU.mult), [b_o, b_small], [b_o])
                            K.op(K.dve, lambda: nc.vector.tensor_scalar(out=o1[:], in0=osb[:, 0, 0:128], scalar1=sc_[:, 0:1], scalar2=None, op0=ALU.mult),
                                 [b_osb, b_o], [b_o])
                            K.op(K.dve, lambda: nc.vector.scalar_tensor_tensor(out=o2[:], in0=osb[:, 1, 0:128], scalar=sc_[:, 2:3], in1=o1[:],
                                                                                op0=ALU.mult, op1=ALU.add), [b_osb, b_o], [b_o])
                            K.op(K.dve, lambda: nc.vector.tensor_tensor(out=sq[:], in0=o2[:], in1=o2[:], op=ALU.mult), [b_o], [b_o])
                            K.op(K.dve, lambda: nc.vector.tensor_reduce(out=sc_[:, 3:4], in_=sq[:], op=ALU.add, axis=mybir.AxisListType.X), [b_o], [b_o])
                            K.op(K.pool, lambda: nc.gpsimd.tensor_scalar(out=sc_[:, 4:5], in0=sc_[:, 3:4], scalar1=1.0 / 128.0, scalar2=SUBLN_EPS, op0=ALU.mult, op1=ALU.add),
                                 [b_o], [b_o])
                            K.op(K.pool, lambda: nc.gpsimd.tensor_tensor(out=sc_[:, 5:6], in0=sc_[:, 4:5], in1=self.eps_t[:, 2:3], op=ALU.pow),
                                 [b_o, self.b_eps], [b_o])
                            K.op(K.dve, lambda: nc.vector.scalar_tensor_tensor(out=ob[:], in0=o2[:], scalar=sc_[:, 5:6], in1=sg[:],
                                                                                op0=ALU.mult, op1=ALU.mult), [b_o, b_sg], [b_o])
                            K.op(K.pe, lambda: nc.tensor.transpose(out=self.pbb[0][:, 0:128], in_=ob[:], identity=self.cn["ident_bf"][:]),
                                 [b_o, self.b_cn["ident_bf"]], [self.b_pbb[0]], mark=True)
                            K.op(K.dve, lambda hd=hd, qt=qt: nc.vector.tensor_copy(out=slT[:, hd, q0 + qt * 128:q0 + (qt + 1) * 128], in_=self.pbb[0][:, 0:128]),
                                 [self.b_pbb[0]], [b_sl])
            K.barrier()
        if ckpt("attn"):
            return
        with ExitStack() as ls:
            lnb = self.ln_bc(ls, i, 1)
            wo = self.sb(ls, [128, 8, D], BF16, "wo")
            b_wo = Buf()
            dsw = K.dsem(f"wo{i}")
            K.dma(K.pool, dsw, wo[:], self.W[("w_out_mix", i)].ap().rearrange("(k p) c -> p k c", p=128), [], [b_wo])
            cmT = self.sb(ls, [128, 4, TOKA], BF16, "cmT")
            b_cmT = Buf()
            dsw = K.dsem(f"cmr{i}")
            K.dma(K.sync, dsw, cmT[:], self.cmd.ap().rearrange("p (c t) -> p c t", c=4), [self.b_cmd], [b_cmT])
            hl = self.sb(ls, [128, 4, 2], BF16, "hl")
            b_hl = Buf()
            hb = self.xh_out.ap()[0:1, 0:1]
            K.dma(K.sync, dsw, hl[:, :, 0:1], bass.AP(hb.tensor, hb.offset + 512, [[1, 128], [128, 4], [1, 1]]), [self.b_xout], [b_hl])
            K.dma(K.sync, dsw, hl[:, :, 1:2], bass.AP(hb.tensor, hb.offset + TOK, [[1, 128], [128, 4], [1, 1]]), [self.b_xout], [b_hl])
            hf_ = self.sb(ls, [128, 4, 2], F32, "hf")
            b_hf = Buf()
            for c in range(4):
                for wi, (tap, colv) in enumerate(((0, 0), (2, TOK - 1))):
                    K.op(K.dve, lambda c=c, wi=wi, tap=tap: nc.vector.tensor_scalar(out=hf_[:, c, wi:wi + 1], in0=hl[:, c, wi:wi + 1],
                                                                                     scalar1=cw[:, c, tap:tap + 1], scalar2=self.cn["halo_mask"][:, wi:wi + 1],
                                                                                     op0=ALU.mult, op1=ALU.mult), [b_hl, b_cw, self.b_cn["halo_mask"]], [b_hf])
                    K.op(K.dve, lambda c=c, wi=wi, colv=colv: nc.vector.scalar_tensor_tensor(out=cmT[:, c, colv:colv + 1], in0=hf_[:, c, wi:wi + 1],
                                                                                              scalar=gbh[:, c, wi:wi + 1], in1=cmT[:, c, colv:colv + 1],
                                                                                              op0=ALU.mult, op1=ALU.add), [b_hf, b_gbh, b_cmT], [b_cmT])
            for t in range(ntu):
                if True:
                    l2 = ls
                    pk = [2 + 2 * (t % 2), 3 + 2 * (t % 2)]
                    for hf in range(2):
                        for k in range(8):
                            if k < 4:
                                lhs = cmT[:, k, t * 128:(t + 1) * 128]
                                rb = b_cmT
                            else:
                                lhs = slT[:, k - 4, t * 128:(t + 1) * 128]
                                rb = b_sl
                            K.op(K.pe, lambda k=k, hf=hf, lhs=lhs: nc.tensor.matmul(
                                self.pb[pk[hf]][:, :], lhsT=lhs, rhs=wo[:, k, hf * 512:(hf + 1) * 512], start=(k == 0), stop=(k == 7)),
                                [rb, b_wo], [self.b_pb[pk[hf]]], mark=(k == 7))
                    self.residual_ln(l2, t, [self.pb[pk[0]][:, :], self.pb[pk[1]][:, :]], [self.b_pb[pk[0]], self.b_pb[pk[1]]],
                                     bc, "c" if t >= NT else "", lnb)
            K.barrier()

    def odd_mixer(self, st, i, j, ctx_upd, ntl, ntu):
        nc, K = self.nc, self.K
        ntl = ntu
        ncols = ntu * 128
        uc = self.sb(st, [128, 2, D], BF16, "uc")
        b_uc = Buf()
        with ExitStack() as ls:
            bc = self.alloc_mods(st, ls, ctx_upd, bc_all=True)
            self.mods(i, 1, bc)
            ub = [self.sb(ls, [128, D], BF16, "ub") for _ in range(2)]
            b_ub = [Buf(), Buf()]
            dsu = K.dsem(f"fu{i}")
            for t in range(ntu):
                if True:
                    l2 = ls
                    if t >= NT:
                        self.modulate_tok(l2, t, bc, "c", uc[:, t - NT, :], b_uc)
                    else:
                        s = t % 2
                        self.modulate_tok(l2, t, bc, "", ub[s][:], b_ub[s])
                        K.dma(K.sync, dsu, self.fin[t // 8].ap()[(t % 8) * 128:(t % 8 + 1) * 128, :], ub[s][:], [b_ub[s]], [self.b_fin])
            K.barrier()
        for k in range(2):
            self.collective(self.fin[k].ap(), self.fout[k].ap(), self.b_fin, self.b_fout)
        with ExitStack() as ls:
            X = [self.sb(ls, [64, 8 * D], BF16, "X") for _ in range(2)]
            b_X = [Buf(), Buf()]
            Y = [self.sb(ls, [128, 8 * D], BF16, "Y") for _ in range(2)]
            b_Y = [Buf(), Buf()]
            dsX = [K.dsem(f"fX{i}{k}") for k in range(2)]
            dsY = K.dsem(f"fY{i}")
            def load_x(nb):
                s = nb % 2
                for rk in range(2):
                    for jj in range(2):
                        fo = self.fout[jj].ap()
                        base = fo[rk * 1024 + nb * 8:rk * 1024 + nb * 8 + 1, 0:1]
                        src = bass.AP(base.tensor, base.offset, [[64 * D, 16], [1, 8 * D]])
                        p0 = rk * 32 + jj * 16
                        K.dma(K.sync, dsX[s], X[s][p0:p0 + 16, :], src, [self.b_fout], [b_X[s]])

            load_x(0)
            load_x(1)
            for nb in range(8):
                s = nb % 2
                for cc in range(16):
                    pbk = cc % 4
                    K.op(K.pe, lambda s=s, cc=cc, pbk=pbk: nc.tensor.matmul(
                        self.pb[pbk][:, :], lhsT=self.cn["t1"][:, :], rhs=X[s][:, cc * 512:(cc + 1) * 512], start=True, stop=True),
                        [self.b_cn["t1"], b_X[s]], [self.b_pb[pbk]], mark=True)
                    eng = K.act if cc % 2 == 0 else K.dve
                    if cc % 2 == 0:
                        K.op(K.act, lambda s=s, cc=cc, pbk=pbk: nc.scalar.copy(out=Y[s][:, cc * 512:(cc + 1) * 512], in_=self.pb[pbk][:, :]),
                             [self.b_pb[pbk]], [b_Y[s]])
                    else:
                        K.op(K.dve, lambda s=s, cc=cc, pbk=pbk: nc.vector.tensor_copy(out=Y[s][:, cc * 512:(cc + 1) * 512], in_=self.pb[pbk][:, :]),
                             [self.b_pb[pbk]], [b_Y[s]])
                if nb + 2 < 8:
                    load_x(nb + 2)
                K.dma(K.sync, dsY, self.yd.ap()[:, nb * 8 * D:(nb + 1) * 8 * D], Y[s][:], [b_Y[s]], [self.b_yd])
            K.barrier()
        def f4(PQ, b_PQ, groups, colbase):
            with ExitStack() as ls:
                lnb = self.ln_bc(ls, i, 1)
                wo = self.sb(ls, [128, 8, D], BF16, "wof")
                b_wo = Buf()
                dsw = K.dsem(f"fw{i}{colbase}")
                K.dma(K.pool, dsw, wo[:], self.W[("w_out_fourier", i)].ap().rearrange("(k p) c -> p k c", p=128), [], [b_wo])
                fT = self.sb(ls, [128, 8, 512], BF16, "fT")
                b_fT = Buf()
                ccs = self.cn["ccs"]
                for gi, (t0, n) in enumerate(groups):
                    w = n * 128
                    cb = t0 * 128 - colbase
                    for g in range(8):
                        pbk = g % 2
                        for pq in range(2):
                            K.op(K.pe, lambda g=g, pq=pq, pbk=pbk: nc.tensor.matmul(
                                self.pb[pbk][:, 0:w], lhsT=ccs[:, pq, :], rhs=PQ[:, pq, g, cb:cb + w], start=(pq == 0), stop=(pq == 1)),
                                [self.b_cn["ccs"], b_PQ], [self.b_pb[pbk]], mark=(pq == 1))
                        K.op(K.act, lambda g=g, pbk=pbk: nc.scalar.copy(out=fT[:, g, 0:w], in_=self.pb[pbk][:, 0:w]), [self.b_pb[pbk]], [b_fT])
                    for tt in range(n):
                        t = t0 + tt
                        pk = [2 + 2 * (tt % 2), 3 + 2 * (tt % 2)]
                        for hf in range(2):
                            for k in range(8):
                                K.op(K.pe, lambda k=k, hf=hf, tt=tt: nc.tensor.matmul(
                                    self.pb[pk[hf]][:, :], lhsT=fT[:, k, tt * 128:(tt + 1) * 128], rhs=wo[:, k, hf * 512:(hf + 1) * 512],
                                    start=(k == 0), stop=(k == 7)), [b_fT, b_wo], [self.b_pb[pk[hf]]], mark=(k == 7))
                        self.residual_ln(ls, t, [self.pb[pk[0]][:, :], self.pb[pk[1]][:, :]], [self.b_pb[pk[0]], self.b_pb[pk[1]]],
                                         bc_g, "c" if t >= NT else "", lnb)
                K.barrier()

        bc_g = bc
        with ExitStack() as lp:
            PQ = self.sb(lp, [128, 2, 8, TOK], BF16, "PQ")
            b_PQ = Buf()
            with ExitStack() as ls:
                G = self.sb(ls, [128, 64, 64], BF16, "G")
                b_G = Buf()
                dsg = K.dsem(f"fG{i}")
                K.dma(K.sync, dsg, G[:], self.Cd["g2"].ap(), [], [b_G])
                Yk = [self.sb(ls, [128, 4, D], BF16, "Yk") for _ in range(2)]
                b_Yk = [Buf(), Buf()]
                dsK = [K.dsem(f"fK{i}{k}") for k in range(2)]
                yda = self.yd.ap()
                for kb in range(16):
                    s = kb % 2
                    for ri in range(2):
                        base = yda[ri * 64 + kb * 4:ri * 64 + kb * 4 + 1, 0:1]
                        src_ = bass.AP(base.tensor, base.offset, [[D, 64], [64 * D, 4], [1, D]])
                        K.dma(K.sync, dsK[s], Yk[s][ri * 64:(ri + 1) * 64, :, :], src_, [self.b_yd], [b_Yk[s]])
                    for g in range(8):
                        pbk = g % 4
                        for kk in range(4):
                            K.op(K.pe, lambda s=s, g=g, kk=kk, pbk=pbk, kb=kb: nc.tensor.matmul(
                                self.pb[pbk][:, kk * 64:(kk + 1) * 64], lhsT=Yk[s][:, kk, g * 128:(g + 1) * 128], rhs=G[:, kb * 4 + kk, :],
                                start=True, stop=True, skip_group_check=True),
                                [b_Yk[s], b_G], [self.b_pb[pbk]], mark=(kk == 3))
                        for pq in range(2):
                            dst = PQ[:, pq, g, 0:TOK].rearrange("p (b a) -> p a b", a=64)[:, kb * 4:(kb + 1) * 4, :]
                            srcp = self.pb[pbk][:, 0:256].rearrange("p (k q b) -> p k q b", k=4, q=2)[:, :, pq, :]
                            if pq == 0:
                                K.op(K.act, lambda dst=dst, srcp=srcp: nc.scalar.copy(out=dst, in_=srcp), [self.b_pb[pbk]], [b_PQ])
                            else:
                                K.op(K.dve, lambda dst=dst, srcp=srcp: nc.vector.tensor_copy(out=dst, in_=srcp), [self.b_pb[pbk]], [b_PQ])
                K.barrier()
            f4(PQ, b_PQ, [(g * 4, 4) for g in range(4)], 0)
        if ctx_upd:
            with ExitStack() as lp:
                PQc = self.sb(lp, [128, 2, 8, 256], BF16, "PQc")
                b_PQc = Buf()
                with ExitStack() as ls:
                    c256 = self.sb(ls, [128, 2, 2, 256], BF16, "c256")
                    b_c256 = Buf()
                    dsc = K.dsem(f"fc{i}")
                    K.dma(K.sync, dsc, c256[:], self.Cd["c256"].ap(), [], [b_c256])
                    for g in range(8):
                        pbk = g % 4
                        for tt in range(2):
                            K.op(K.pe, lambda g=g, tt=tt, pbk=pbk: nc.tensor.matmul(
                                self.pb[pbk][:, :], lhsT=uc[:, tt, g * 128:(g + 1) * 128], rhs=c256[:, tt, :, :].rearrange("p a b -> p (a b)"),
                                start=(tt == 0), stop=(tt == 1)), [b_uc, b_c256], [self.b_pb[pbk]], mark=(tt == 1))
                        K.op(K.act, lambda g=g, pbk=pbk: nc.scalar.copy(out=PQc[:, :, g, :],
                                                                        in_=self.pb[pbk][:].rearrange("p (a b) -> p a b", a=2)), [self.b_pb[pbk]], [b_PQc])
                    K.barrier()
                f4(PQc, b_PQc, [(16, 2)], TOK)


_PROGS = {}
LAYER_GROUPS = [[0, 1, 2, 3]]


def _get_prog(layers):
    key = tuple(layers)
    if key not in _PROGS:
        _PROGS[key] = Prog(list(layers))
    return _PROGS[key]


def run_layers(inputs, layer_groups, h=None, hc=None):
    x = np.asarray(inputs["x"], np.float32)
    B = x.shape[0]
    if h is None:
        h = x
        hc = np.asarray(inputs["ctx"], np.float32)
    consts = [make_consts(0), make_consts(1)]
    full = {n: np.asarray(inputs[n], np.float32) for n in W_SPECS}
    c = np.asarray(inputs["c"], np.float32)
    c_ctx = np.asarray(inputs["c_ctx"], np.float32)
    for layers in layer_groups:
        prog = _get_prog(layers)
        in_maps = []
        for r in range(8):
            b, half = r // 2, r % 2
            m = {"x_in": np.ascontiguousarray(h[b, half * TOK:(half + 1) * TOK]),
                 "hc_in": np.ascontiguousarray(hc[b]),
                 "cvec": np.ascontiguousarray(np.stack([c[b], c_ctx], 0))}
            for (n, i), tns in prog.W.items():
                per = 2 if n in ("w_in", "conv_w", "lambda_qk", "subln_g", "w_out_mix", "w_out_fourier") else 1
                m[tns.name] = np.ascontiguousarray(full[n][i // per])
            m.update(consts[half])
            in_maps.append(m)
        res = run_bass_kernel_spmd(prog.nc, in_maps, core_ids=list(range(8)))
        h = np.stack([np.concatenate([res.results[2 * b]["h_out"], res.results[2 * b + 1]["h_out"]], 0) for b in range(B)], 0)
        hc = np.stack([res.results[2 * b]["hc_out"] for b in range(B)], 0)
    return h, hc


def kernel(**inputs):
    h, _ = run_layers(inputs, LAYER_GROUPS)
    return np.ascontiguousarray(h.astype(np.float32))
```
